# Optimizing a Trainium2 kernel written in Bass

```python
import math
import jax, jax.numpy as jnp
from jax import lax
import numpy as np

D_MODEL = 2048
BATCH = 2
SEQ = 8192
DEPTH = 2

N_META = 16
CHUNK = 128
N_BRANCH = 4
BRANCH_W = D_MODEL // 4
EPS = 1e-6
LRU_BLOCKS = 8
LRU_BLOCK_DIM = BRANCH_W // LRU_BLOCKS
LRU_CONV = 4
LRU_PAD = (2, 1)
LRU_C = 8.0
HY_CONV = 3
HY_PAD = (1, 1)
HY_BANDS = 16
HY_EMB = 1 + 2 * HY_BANDS
HY_FFN = 64
HY_SIN_FREQ = 1.0
HY_DECAY_MIN = 3.07
HY_DECAY_MAX = 15.35
HY_FILTER_SCALE = 0.004
RET_HEADS = 4
RET_DK = BRANCH_W // RET_HEADS
RET_DV = BRANCH_W // RET_HEADS
ROPE_BASE = 10000.0
HG_HEADS = 4
HG_EXPAND = BRANCH_W // HG_HEADS
D_FF = -(-8 * D_MODEL // (3 * 256)) * 256
GATE_COLS = N_BRANCH * D_MODEL
MIX_COLS = 14 * BRANCH_W
N_IN_COLS = MIX_COLS + GATE_COLS
F32 = jnp.float32

kernel_name = "hybrid_rglru_hyena_retention_hgrn2_encoder"


def rms_norm(x, gain):
    xf = x.astype(F32)
    y = xf * lax.rsqrt(jnp.mean(xf * xf, axis=-1, keepdims=True) + EPS)
    return (y * gain.astype(F32)).astype(x.dtype)


def depthwise_conv(x, w, b, pad):
    y = lax.conv_general_dilated(x, w[:, None, :].astype(x.dtype), window_strides=(1,), padding=[pad],
                                 dimension_numbers=('NWC', 'WIO', 'NWC'), feature_group_count=x.shape[-1])
    return y + b.astype(x.dtype)


def split_projection(p):
    w = BRANCH_W
    sizes = (w, w, 3 * w, w, w, w, w, w, w, w, w, w, GATE_COLS)
    out, start = [], 0
    for s in sizes:
        out.append(p[..., start:start + s])
        start += s
    return out


def rglru_direction(xc, wa, ba, wx, bx, lam, reverse):
    bsz, t_len, width = xc.shape
    xb = xc.reshape(bsz, t_len, LRU_BLOCKS, LRU_BLOCK_DIM)
    gate_r = jax.nn.sigmoid(jnp.einsum('btki,kij->btkj', xb, wa).reshape(bsz, t_len, width).astype(F32) + ba.astype(F32))
    gate_i = jax.nn.sigmoid(jnp.einsum('btki,kij->btkj', xb, wx).reshape(bsz, t_len, width).astype(F32) + bx.astype(F32))
    log_a = -LRU_C * gate_r * jax.nn.softplus(-lam.astype(F32))
    a = jnp.exp(log_a)
    b = jnp.sqrt(-jnp.expm1(2.0 * log_a)) * gate_i * xc.astype(F32)

    def combine(left, right):
        a1, b1 = left
        a2, b2 = right
        return a1 * a2, a2 * b1 + b2

    _, h = lax.associative_scan(combine, (a, b), axis=1, reverse=reverse)
    return h


def rglru_branch(xa, ga, conv_w, conv_b, wa, ba, wx, bx, lam):
    xc = depthwise_conv(xa, conv_w, conv_b, LRU_PAD)
    h = (rglru_direction(xc, wa[0], ba[0], wx[0], bx[0], lam[0], False)
         + rglru_direction(xc, wa[1], ba[1], wx[1], bx[1], lam[1], True))
    return (h * jax.nn.gelu(ga.astype(F32))).astype(xa.dtype)


def hyena_filters(t_len, w1, b1, w2, b2, w3, decay):
    n = jnp.arange(t_len, dtype=F32)
    t = n / max(t_len - 1, 1)
    freqs = jnp.linspace(1e-4, HY_BANDS - 1, HY_BANDS, dtype=F32)
    ang = 2.0 * math.pi * n[:, None] * freqs[None, :] / t_len
    z = jnp.concatenate([t[:, None], jnp.cos(ang), -jnp.sin(ang)], axis=-1)
    h = jnp.sin(HY_SIN_FREQ * (z @ w1.astype(F32) + b1.astype(F32)))
    h = jnp.sin(HY_SIN_FREQ * (h @ w2.astype(F32) + b2.astype(F32)))
    h = h @ w3.astype(F32)
    return h * jnp.exp(-t[:, None] * jnp.abs(decay.astype(F32))[None, :])


def hyena_branch(u3, conv_w, conv_b, w1, b1, w2, b2, w3, decay, bias):
    u3 = depthwise_conv(u3, conv_w, conv_b, HY_PAD)
    x0, x1, v = jnp.split(u3, 3, axis=-1)
    t_len = u3.shape[1]
    filt = hyena_filters(t_len, w1, b1, w2, b2, w3, decay)
    h_fwd, h_bwd = filt[:, :BRANCH_W], filt[:, BRANCH_W:]
    k2 = jnp.concatenate([h_fwd, jnp.zeros((1, BRANCH_W), F32), h_bwd[:0:-1]], axis=0)
    u = (x1 * v).astype(F32)
    n_fft = 2 * t_len
    y = jnp.fft.irfft(jnp.fft.rfft(u, n=n_fft, axis=1) * jnp.fft.rfft(k2, n=n_fft, axis=0)[None],
                      n=n_fft, axis=1)[:, :t_len]
    y = y + u * bias.astype(F32)
    return (x0.astype(F32) * y).astype(u3.dtype)


def to_heads(x, n_heads):
    bsz, t_len, _ = x.shape
    return x.reshape(bsz, t_len, n_heads, -1).transpose(0, 2, 1, 3)


def from_heads(x):
    bsz, n_heads, t_len, d = x.shape
    return x.transpose(0, 2, 1, 3).reshape(bsz, t_len, n_heads * d)


def rotary(x):
    t_len, d = x.shape[2], x.shape[3]
    inv = ROPE_BASE ** (-jnp.arange(0, d, 2, dtype=F32) / d)
    ang = jnp.arange(t_len, dtype=F32)[:, None] * inv[None, :]
    cos, sin = jnp.cos(ang), jnp.sin(ang)
    xa, xb = x[..., : d // 2], x[..., d // 2:]
    return jnp.concatenate([xa * cos - xb * sin, xb * cos + xa * sin], axis=-1)


def gla_chunk_scan(q, k, v, log_f, inclusive):
    bsz, n_heads, t_len, dk = q.shape
    dv = v.shape[-1]
    n_chunks = t_len // CHUNK

    def chunks(a):
        return a.reshape(bsz, n_heads, n_chunks, CHUNK, a.shape[-1]).transpose(2, 0, 1, 3, 4)

    qc, kc, vc = chunks(q), chunks(k), chunks(v)
    bc = jnp.cumsum(chunks(log_f), axis=-2)
    idx = jnp.arange(CHUNK)
    mask = (idx[:, None] >= idx[None, :]) if inclusive else (idx[:, None] > idx[None, :])

    def step(state, inp):
        q_, k_, v_, b_ = inp
        diff = b_[..., :, None, :] - b_[..., None, :, :]
        dec = jnp.exp(jnp.where(mask[:, :, None], diff, -jnp.inf))
        scores = jnp.einsum('bhid,bhjd,bhijd->bhij', q_, k_, dec)
        out = (jnp.einsum('bhij,bhjv->bhiv', scores, v_)
               + jnp.einsum('bhid,bhdv->bhiv', q_ * jnp.exp(b_), state))
        b_last = b_[..., -1:, :]
        state = (jnp.exp(b_last[..., 0, :])[..., None] * state
                 + jnp.einsum('bhjd,bhjv->bhdv', k_ * jnp.exp(b_last - b_), v_))
        return state, out

    s0 = jnp.zeros((bsz, n_heads, dk, dv), F32)
    _, o = lax.scan(step, s0, (qc, kc, vc, bc))
    return o.transpose(1, 2, 0, 3, 4).reshape(bsz, n_heads, t_len, dv)


def bidirectional_gla(q, k_fwd, k_bwd, v, logf_fwd, logf_bwd, inclusive_bwd):
    pad = CHUNK - N_META

    def padf(a):
        return jnp.pad(a, ((0, 0), (0, 0), (pad, 0), (0, 0)))

    def flipf(a):
        return jnp.flip(padf(a), axis=2)

    fwd = gla_chunk_scan(padf(q), padf(k_fwd), padf(v), padf(logf_fwd), True)
    bwd = jnp.flip(gla_chunk_scan(flipf(q), flipf(k_bwd), flipf(v), flipf(logf_bwd), inclusive_bwd), axis=2)
    return (fwd + bwd)[:, :, pad:]


def retention_branch(q, k, v, g):
    qh = rotary(to_heads(q.astype(F32), RET_HEADS))
    kh = rotary(to_heads(k.astype(F32), RET_HEADS)) * (RET_DK ** -0.5)
    vh = to_heads(v.astype(F32), RET_HEADS)
    log_gamma = jnp.log1p(-(2.0 ** (-5.0 - jnp.arange(RET_HEADS, dtype=F32))))
    logf = jnp.broadcast_to(log_gamma[None, :, None, None], qh.shape)
    o = bidirectional_gla(qh, kh, kh, vh, logf, logf, inclusive_bwd=False)
    mu = jnp.mean(o, axis=-1, keepdims=True)
    o = (o - mu) * lax.rsqrt(jnp.mean((o - mu) ** 2, axis=-1, keepdims=True) + EPS)
    return (from_heads(o) * jax.nn.silu(g.astype(F32))).astype(q.dtype)


def hgrn_lower_bound(lb_logits, layer):
    c = jnp.cumsum(jax.nn.softmax(lb_logits.astype(F32), axis=0), axis=0)
    return c[layer] - c[0]


def hgrn2_branch(q, f_fwd, f_bwd, i, g, lb):
    qh = to_heads(jax.nn.silu(q.astype(F32)), HG_HEADS)
    ff = lb + (1.0 - lb) * jax.nn.sigmoid(f_fwd.astype(F32))
    fb = lb + (1.0 - lb) * jax.nn.sigmoid(f_bwd.astype(F32))
    o = bidirectional_gla(qh, to_heads(1.0 - ff, HG_HEADS), to_heads(1.0 - fb, HG_HEADS),
                          to_heads(i.astype(F32), HG_HEADS),
                          to_heads(jnp.log(ff), HG_HEADS), to_heads(jnp.log(fb), HG_HEADS), inclusive_bwd=True)
    o = o * lax.rsqrt(jnp.mean(o * o, axis=-1, keepdims=True) + EPS)
    return (from_heads(o) * jax.nn.silu(g.astype(F32))).astype(q.dtype)


def mixer_block(n, w_in, lru_conv_w, lru_conv_b, lru_wa, lru_ba, lru_wx, lru_bx, lru_lambda,
                hy_conv_w, hy_conv_b, hy_w1, hy_b1, hy_w2, hy_b2, hy_w3, hy_decay, hy_bias,
                lb, w_branch_out, w_out):
    bsz, t_len, _ = n.shape
    p = n @ w_in
    (a_x, a_g, b_u, c_q, c_k, c_v, c_g, d_q, d_ff, d_fb, d_i, d_g, gate_cols) = split_projection(p)
    za = rglru_branch(a_x, a_g, lru_conv_w, lru_conv_b, lru_wa, lru_ba, lru_wx, lru_bx, lru_lambda)
    zb = hyena_branch(b_u, hy_conv_w, hy_conv_b, hy_w1, hy_b1, hy_w2, hy_b2, hy_w3, hy_decay, hy_bias)
    zc = retention_branch(c_q, c_k, c_v, c_g)
    zd = hgrn2_branch(d_q, d_ff, d_fb, d_i, d_g, lb)
    z = jnp.stack([za, zb, zc, zd], axis=2)
    up = jnp.einsum('btnw,nwd->btnd', z, w_branch_out)
    gates = jax.nn.sigmoid(gate_cols.reshape(bsz, t_len, N_BRANCH, D_MODEL).astype(F32))
    merged = jnp.sum(gates * up.astype(F32), axis=2).astype(n.dtype)
    return merged @ w_out


def swiglu(n, w_gate, w_up, w_down):
    return (jax.nn.silu(n @ w_gate) * (n @ w_up)) @ w_down


def setup_inputs(seed: int = 0) -> dict:
    key = jax.random.key(seed)
    ks = jax.random.split(key, 28)
    L, W = DEPTH, BRANCH_W

    def normal(k, shape, scale):
        return jax.random.normal(k, shape, F32) * scale

    u_lam = jax.random.uniform(ks[11], (L, 2, W), F32, minval=0.9, maxval=0.999)
    a_lam = u_lam ** (1.0 / LRU_C)
    decay0 = jnp.tile(jnp.linspace(HY_DECAY_MIN, HY_DECAY_MAX, W, dtype=F32), 2)
    return {
        'x': normal(ks[0], (BATCH, SEQ, D_MODEL), 1.0),
        'meta': normal(ks[1], (N_META, D_MODEL), 1.0),
        'norm_mix': 1.0 + normal(ks[2], (L, D_MODEL), 0.02),
        'norm_ffn': 1.0 + normal(ks[3], (L, D_MODEL), 0.02),
        'w_in': normal(ks[4], (L, D_MODEL, N_IN_COLS), D_MODEL ** -0.5),
        'lru_conv_w': normal(ks[5], (L, LRU_CONV, W), LRU_CONV ** -0.5),
        'lru_conv_b': normal(ks[6], (L, W), 0.02),
        'lru_wa': normal(ks[7], (L, 2, LRU_BLOCKS, LRU_BLOCK_DIM, LRU_BLOCK_DIM), LRU_BLOCK_DIM ** -0.5),
        'lru_ba': normal(ks[8], (L, 2, W), 0.02),
        'lru_wx': normal(ks[9], (L, 2, LRU_BLOCKS, LRU_BLOCK_DIM, LRU_BLOCK_DIM), LRU_BLOCK_DIM ** -0.5),
        'lru_bx': normal(ks[10], (L, 2, W), 0.02),
        'lru_lambda': jnp.log(a_lam) - jnp.log1p(-a_lam),
        'hy_conv_w': normal(ks[12], (L, HY_CONV, 3 * W), HY_CONV ** -0.5),
        'hy_conv_b': normal(ks[13], (L, 3 * W), 0.02),
        'hy_w1': normal(ks[14], (L, HY_EMB, HY_FFN), HY_EMB ** -0.5),
        'hy_b1': normal(ks[15], (L, HY_FFN), 0.1),
        'hy_w2': normal(ks[16], (L, HY_FFN, HY_FFN), HY_FFN ** -0.5),
        'hy_b2': normal(ks[17], (L, HY_FFN), 0.1),
        'hy_w3': normal(ks[18], (L, HY_FFN, 2 * W), HY_FILTER_SCALE),
        'hy_decay': decay0[None, :] + normal(ks[19], (L, 2 * W), 0.1),
        'hy_bias': normal(ks[20], (L, W), 0.5),
        'hgrn_lb_logits': normal(ks[21], (L, W), 0.1),
        'w_branch_out': normal(ks[22], (L, N_BRANCH, W, D_MODEL), W ** -0.5),
        'w_out': normal(ks[23], (L, D_MODEL, D_MODEL), D_MODEL ** -0.5),
        'ffn_w_gate': normal(ks[24], (L, D_MODEL, D_FF), D_MODEL ** -0.5),
        'ffn_w_up': normal(ks[25], (L, D_MODEL, D_FF), D_MODEL ** -0.5),
        'ffn_w_down': normal(ks[26], (L, D_FF, D_MODEL), D_FF ** -0.5),
        'norm_final': 1.0 + normal(ks[27], (D_MODEL,), 0.02),
    }


def reference(x, meta, norm_mix, norm_ffn, w_in, lru_conv_w, lru_conv_b, lru_wa, lru_ba, lru_wx, lru_bx,
              lru_lambda, hy_conv_w, hy_conv_b, hy_w1, hy_b1, hy_w2, hy_b2, hy_w3, hy_decay, hy_bias,
              hgrn_lb_logits, w_branch_out, w_out, ffn_w_gate, ffn_w_up, ffn_w_down, norm_final):
    bsz = x.shape[0]
    h = jnp.concatenate([jnp.broadcast_to(meta[None].astype(x.dtype), (bsz, N_META, D_MODEL)), x], axis=1)
    for l in range(DEPTH):
        lb = hgrn_lower_bound(hgrn_lb_logits, l)
        h = h + mixer_block(rms_norm(h, norm_mix[l]), w_in[l], lru_conv_w[l], lru_conv_b[l], lru_wa[l],
                            lru_ba[l], lru_wx[l], lru_bx[l], lru_lambda[l], hy_conv_w[l], hy_conv_b[l],
                            hy_w1[l], hy_b1[l], hy_w2[l], hy_b2[l], hy_w3[l], hy_decay[l], hy_bias[l],
                            lb, w_branch_out[l], w_out[l])
        h = h + swiglu(rms_norm(h, norm_ffn[l]), ffn_w_gate[l], ffn_w_up[l], ffn_w_down[l])
    return rms_norm(h, norm_final)[:, N_META:]
```

```python
import contextlib
import numpy as np
import concourse.bass as bass
import concourse.mybir as mybir
from concourse.bass_utils import run_bass_kernel_spmd

F32 = mybir.dt.float32
BF16 = mybir.dt.bfloat16
I32 = mybir.dt.int32
AF = mybir.ActivationFunctionType
ALU = mybir.AluOpType
AX = mybir.AxisListType

ENGS = ("sync", "act", "dve", "pe", "pool")


class Prog:
    NDMASEM = 6

    def __init__(self, nc):
        self.nc = nc
        self.ops = []
        self.last_w = {}
        self.readers = {}
        self.last_eng = {}
        self.pending = {}

    def op(self, eng, fn, reads=(), writes=(), dma=False):
        i = len(self.ops)
        deps = set()
        for k in reads:
            if k in self.last_w:
                deps.add(self.last_w[k])
        for k in writes:
            if k in self.last_w:
                deps.add(self.last_w[k])
            for r in self.readers.get(k, ()):
                deps.add(r)
        if eng in self.pending:
            deps |= self.pending.pop(eng)
        deps.discard(i)
        self.last_eng[eng] = i
        self.ops.append(dict(eng=eng, fn=fn, deps=deps, dma=dma, signal=False))
        for k in reads:
            self.readers.setdefault(k, []).append(i)
        for k in writes:
            self.last_w[k] = i
            self.readers[k] = []
        return i

    def barrier(self):
        allp = set(self.last_eng.values()) | {i for i, o in enumerate(self.ops) if o["dma"]}
        for e in ENGS:
            self.pending[e] = set(allp) | self.pending.get(e, set())

    def dma(self, eng, out, in_, reads=(), writes=(), **kw):
        return self.op(eng, lambda e: e.dma_start(out=out, in_=in_, **kw), reads, writes, dma=True)

    def emit(self):
        nc = self.nc
        ops = self.ops
        dma_count = {e: 0 for e in ENGS}
        for i, o in enumerate(ops):
            if o["dma"]:
                j = dma_count[o["eng"]]
                o["dma_idx"] = j
                dma_count[o["eng"]] += 1
        dma_ops = {e: [i for i, o in enumerate(ops) if o["dma"] and o["eng"] == e] for e in ENGS}
        for e in ENGS:
            lst = dma_ops[e]
            for j, i in enumerate(lst):
                if j >= self.NDMASEM:
                    ops[i]["deps"].add(lst[j - self.NDMASEM])
        for o in ops:
            for d in o["deps"]:
                po = ops[d]
                if po["dma"]:
                    continue
                if po["eng"] == "pe" and o["eng"] == "pe" and not o["dma"]:
                    continue
                po["signal"] = True
        cnt = {e: 0 for e in ENGS}
        for o in ops:
            if o["dma"]:
                j = o["dma_idx"]
                o["sem"] = ("dma", o["eng"], j % self.NDMASEM)
                o["val"] = 16 * (j // self.NDMASEM + 1)
            elif o["signal"]:
                cnt[o["eng"]] += 1
                o["sem"] = ("eng", o["eng"])
                o["val"] = cnt[o["eng"]]
        used = [e for e in ENGS if any(o["eng"] == e for o in ops)]
        import contextlib
        with contextlib.ExitStack() as st:
            sems = {}
            for e in used:
                sems[("eng", e)] = st.enter_context(nc.semaphore("s_" + e))
                if dma_count[e]:
                    for k in range(min(self.NDMASEM, dma_count[e])):
                        sems[("dma", e, k)] = st.enter_context(nc.semaphore("d_%s_%d" % (e, k)))
            block = st.enter_context(nc.Block())

            def make(e):
                def body(eng):
                    waited = {}
                    for o in ops:
                        if o["eng"] != e:
                            continue
                        need = {}
                        for d in o["deps"]:
                            po = ops[d]
                            if "sem" not in po:
                                continue
                            s = po["sem"]
                            need[s] = max(need.get(s, 0), po["val"])
                        for s, v in need.items():
                            if waited.get(s, 0) >= v:
                                continue
                            eng.wait_ge(sems[s], v)
                            waited[s] = v
                        ins = o["fn"](eng)
                        if o["dma"]:
                            ins.then_inc(sems[o["sem"]], 16)
                        elif o["signal"]:
                            ins.then_inc(sems[o["sem"]], 1)
                    for k in range(min(self.NDMASEM, dma_count[e])):
                        lastv = 0
                        for i in dma_ops[e]:
                            if ops[i]["sem"] == ("dma", e, k):
                                lastv = ops[i]["val"]
                        if waited.get(("dma", e, k), 0) < lastv:
                            eng.wait_ge(sems[("dma", e, k)], lastv)
                return body

            reg = dict(sync=block.sync, act=block.scalar, dve=block.vector, pe=block.tensor, pool=block.gpsimd)
            for e in used:
                reg[e](make(e))


def _keys(items):
    out = []
    for it in items:
        if isinstance(it, (str, tuple)):
            out.append(it)
        elif hasattr(it, "tensor"):
            out.append(it.tensor.name)
        else:
            out.append(it.name)
    return out


class PX(Prog):
    def _rw(self, r, w, dr, dw):
        return _keys(dr if r is None else r), _keys(dw if w is None else w)

    def ld(self, eng, out, in_, r=None, w=None, **kw):
        rr, ww = self._rw(r, w, [in_], [out])
        return self.op(eng, lambda e: e.dma_start(out=out, in_=in_, **kw), rr, ww, dma=True)

    def mm(self, out, lhsT, rhs, start=True, stop=True, r=None, w=None, **kw):
        rr, ww = self._rw(r, w, [lhsT, rhs], [out])
        return self.op("pe", lambda e: e.matmul(out, lhsT=lhsT, rhs=rhs, start=start, stop=stop, **kw), rr, ww)

    def tr(self, out, in_, ident, r=None, w=None):
        rr, ww = self._rw(r, w, [in_, ident], [out])
        return self.op("pe", lambda e: e.transpose(out, in_, ident), rr, ww)

    def act(self, out, in_, func, bias=None, scale=None, accum_out=None, r=None, w=None, eng="act"):
        kw = {}
        rd = [in_]
        wd = [out]
        if bias is not None:
            kw["bias"] = bias
            if not isinstance(bias, (int, float)):
                rd.append(bias)
        if scale is not None:
            kw["scale"] = scale
            if not isinstance(scale, (int, float)):
                rd.append(scale)
        if accum_out is not None:
            kw["accum_out"] = accum_out
            wd.append(accum_out)
        rr, ww = self._rw(r, w, rd, wd)
        return self.op(eng, lambda e: e.activation(out=out, in_=in_, func=func, **kw), rr, ww)

    def tt(self, out, in0, in1, op, eng="dve", r=None, w=None):
        rr, ww = self._rw(r, w, [in0, in1], [out])
        return self.op(eng, lambda e: e.tensor_tensor(out=out, in0=in0, in1=in1, op=op), rr, ww)

    def ts(self, out, in0, s1, s2, op0, op1=None, eng="dve", r=None, w=None, accum_out=None):
        rd = [in0] + [s for s in (s1, s2) if s is not None and not isinstance(s, (int, float))]
        wd = [out] + ([accum_out] if accum_out is not None else [])
        rr, ww = self._rw(r, w, rd, wd)
        kw = {}
        if op1 is not None:
            kw["op1"] = op1
        if accum_out is not None:
            kw["accum_out"] = accum_out
        return self.op(eng, lambda e: e.tensor_scalar(out=out, in0=in0, scalar1=s1, scalar2=s2, op0=op0, **kw), rr, ww)

    def stt(self, out, in0, scalar, in1, op0, op1, eng="dve", r=None, w=None):
        rd = [in0, in1] + ([] if isinstance(scalar, (int, float)) else [scalar])
        rr, ww = self._rw(r, w, rd, [out])
        return self.op(eng, lambda e: e.scalar_tensor_tensor(out=out, in0=in0, scalar=scalar, in1=in1, op0=op0, op1=op1), rr, ww)

    def scan(self, out, d0, d1, init=0.0, op0=None, op1=None, r=None, w=None):
        rr, ww = self._rw(r, w, [d0, d1], [out])
        return self.op("dve", lambda e: e.tensor_tensor_scan(out=out, data0=d0, data1=d1, initial=init, op0=op0 or ALU.mult, op1=op1 or ALU.add), rr, ww)

    def cp(self, out, in_, eng="dve", r=None, w=None):
        rr, ww = self._rw(r, w, [in_], [out])
        if eng == "act":
            return self.op(eng, lambda e: e.copy(out=out, in_=in_), rr, ww)
        return self.op(eng, lambda e: e.tensor_copy(out=out, in_=in_), rr, ww)

    def recip(self, out, in_, r=None, w=None):
        rr, ww = self._rw(r, w, [in_], [out])
        return self.op("dve", lambda e: e.reciprocal(out=out, in_=in_), rr, ww)

    def memset(self, out, val, eng="dve", w=None):
        rr, ww = self._rw([], w, [], [out])
        return self.op(eng, lambda e: e.memset(out, val), rr, ww)


def rev_ap(ap):
    (ps, pn), (fs, fn) = ap.ap
    return bass.AP(ap.tensor, ap.offset + fs * (fn - 1), [[ps, pn], [-fs, fn]])


T = 8208
TP = 8320
PADL = 112
NT = 65
NBUF = 5
BW = 8320


def alloc_big(nc, st):
    return [st.enter_context(nc.sbuf_tensor("BIG%d" % i, [128, BW], F32)) for i in range(NBUF)]


def mixer_A(P, nc, st, big, psb, sm, pc, prm, zout, q="sync"):
    X, XC, A_, B_, HB = [b for b in big]
    sb = lambda name, shape, dt=F32: st.enter_context(nc.sbuf_tensor("s_" + name, shape, dt))
    cw = sb("a_cw", [128, 4]); cb = sb("a_cb", [128, 1])
    wa = sm["p512"][0][:, 0:256].rearrange("p (a b) -> p a b", a=2); wx = sm["p512"][1][:, 0:256].rearrange("p (a b) -> p a b", a=2)
    ba = sb("a_ba", [128, 2]); bx = sb("a_bx", [128, 2]); lam = sb("a_lam", [128, 2]); cch = sb("a_cch", [128, 2])
    gr, gi, sq = sm["p512"][2:5]
    ps_r = psb[0:2]
    ps_i = psb[2:4]
    P.ld(q, cw[:], prm["a_cw"]); P.ld(q, cb[:], prm["a_cb"])
    P.ld(q, wa, prm["a_wa"]); P.ld(q, wx, prm["a_wx"])
    P.ld(q, ba[:], prm["a_ba"]); P.ld(q, bx[:], prm["a_bx"]); P.ld(q, lam[:], prm["a_lam"])
    P.act(cch[:], lam[:], AF.Exp, scale=-1.0)
    P.act(cch[:], cch[:], AF.Ln, bias=1.0)
    P.ts(cch[:], cch[:], -8.0, None, ALU.mult)
    P.memset(X[:, 0:2], 0.0); P.memset(X[:, 2 + T:2 + T + 2], 0.0)
    P.ld(q, X[:, 2:2 + T], pc[0])
    P.act(XC[:, 0:T], X[:, 2:2 + T], AF.Identity, bias=cb[:, 0:1], scale=cw[:, 2:3])
    for k, off in ((0, 0), (1, 1), (3, 3)):
        P.stt(XC[:, 0:T], X[:, off:off + T], cw[:, k:k + 1], XC[:, 0:T], ALU.mult, ALU.add)
    ntile = (T + 511) // 512
    for d in range(2):
        for ti in range(ntile):
            t0 = ti * 512; n = min(512, T - t0); j = ti % 2
            P.mm(ps_r[j][:, 0:n], wa[:, d, :], XC[:, t0:t0 + n])
            P.mm(ps_i[j][:, 0:n], wx[:, d, :], XC[:, t0:t0 + n])
            P.act(gr[:, 0:n], ps_r[j][:, 0:n], AF.Sigmoid, bias=ba[:, d:d + 1])
            P.act(gi[:, 0:n], ps_i[j][:, 0:n], AF.Sigmoid, bias=bx[:, d:d + 1])
            P.act(A_[:, t0:t0 + n], gr[:, 0:n], AF.Exp, scale=cch[:, d:d + 1])
            P.act(sq[:, 0:n], A_[:, t0:t0 + n], AF.Square)
            P.act(sq[:, 0:n], sq[:, 0:n], AF.Sqrt, bias=1.0, scale=-1.0)
            P.tt(gi[:, 0:n], gi[:, 0:n], sq[:, 0:n], ALU.mult)
            P.tt(B_[:, t0:t0 + n], gi[:, 0:n], XC[:, t0:t0 + n], ALU.mult)
        if d == 0:
            P.scan(X[:, 0:T], A_[:, 0:T], B_[:, 0:T])
        else:
            P.scan(rev_ap(HB[:, 0:T]), rev_ap(A_[:, 0:T]), rev_ap(B_[:, 0:T]))
    P.tt(X[:, 0:T], X[:, 0:T], HB[:, 0:T], ALU.add)
    P.ld(q, A_[:, 0:T], pc[1])
    P.act(B_[:, 0:T], A_[:, 0:T], AF.Gelu_apprx_tanh)
    P.tt(X[:, 0:T], X[:, 0:T], B_[:, 0:T], ALU.mult)
    P.ld(q, zout, X[:, 0:T])


def prep_A(inp, l, g):
    sl = slice(g * 128, (g + 1) * 128)
    d = {}
    d["a_cw"] = np.ascontiguousarray(inp["lru_conv_w"][l][:, sl].T)
    d["a_cb"] = np.ascontiguousarray(inp["lru_conv_b"][l][sl][:, None])
    for nm, key in (("a_wa", "lru_wa"), ("a_wx", "lru_wx")):
        w = np.zeros((128, 2, 128), np.float32)
        for dd in range(2):
            for kb in range(2):
                w[kb * 64:(kb + 1) * 64, dd, kb * 64:(kb + 1) * 64] = inp[key][l][dd, 2 * g + kb]
        d[nm] = w
    d["a_ba"] = np.ascontiguousarray(inp["lru_ba"][l][:, sl].T)
    d["a_bx"] = np.ascontiguousarray(inp["lru_bx"][l][:, sl].T)
    d["a_lam"] = np.ascontiguousarray(inp["lru_lambda"][l][:, sl].T)
    return d


def alloc_small(nc, st):
    p128 = [st.enter_context(nc.sbuf_tensor("SM%d" % i, [128, 128], F32)) for i in range(36)]
    p512 = [st.enter_context(nc.sbuf_tensor("SL%d" % i, [128, 512], F32)) for i in range(5)]
    rmb = st.enter_context(nc.sbuf_tensor("RMB", [128, TP], BF16))
    return dict(p128=p128, p512=p512, rmb=rmb)


class Pool128:
    def __init__(self, lst):
        self.lst = list(lst); self.i = 0

    def get(self, n=1):
        out = self.lst[self.i:self.i + n]; self.i += n
        assert self.i <= len(self.lst)
        return out if n > 1 else out[0]


def alloc_psum(nc, st):
    return [st.enter_context(nc.psum_tensor("PSB%d" % i, [128, 512], F32)) for i in range(8)]


def load_tok_padded(P, q, dst3, src):
    P.memset(dst3[0:PADL, 0, :], 0.0, w=[dst3])
    P.ld(q, dst3[PADL:128, 0, :], src[0:16, :], w=[dst3])
    P.ld(q, dst3[:, 1:NT, :], src[16:T, :].rearrange("(n p) d -> p n d", p=128), w=[dst3])


def mixer_C(P, nc, st, big, psb, sm, pc, pt, prm, zout, q="sync"):
    Q, KS, SBALL, CS, K = big
    pl = Pool128(sm["p128"])
    sb = lambda name, shape, dt=F32: st.enter_context(nc.sbuf_tensor("s_" + name, shape, dt))
    cm = sm["p512"][0][:, 0:384].rearrange("p (a b) -> p a b", a=3); cv = sb("c_v", [128, 4]); ident = pl.get()
    kbs = pl.get(2); pt_ = pl.get(2); qf = pl.get(2); qb = pl.get(2); sf = pl.get(2); osb = pl.get(2)
    junk = pl.get(); gt = pl.get(2); vt = pl.get(3)
    st4 = [sb("c_st%d" % i, [128, 4]) for i in range(2)]
    P.ld(q, cm, prm["c_m"]); P.ld(q, cv[:], prm["c_vec"]); P.ld(q, ident[:], prm["ident"])

    def ldv(i, n_):
        vj = n_ % 3
        if i == 0:
            P.memset(vt[vj][:], 0.0)
            P.ld("act", vt[vj][PADL:128, :], pt[0][0:16, :])
        else:
            P.ld("act", vt[vj][:], pt[0][16 + (i - 1) * 128:16 + i * 128, :])
        return vt[vj]
    SB3 = SBALL[:, 0:NT * 128].rearrange("p (n d) -> p n d", d=128)
    for dst, src in ((Q, pc[5]), (K, pc[6])):
        P.memset(dst[:, 0:PADL], 0.0); P.memset(KS[:, 0:PADL], 0.0)
        P.ld(q, dst[:, PADL:TP], src)
        P.ld(q, KS[0:64, PADL:TP], src[64:128, :], w=[KS]); P.ld("act", KS[64:128, PADL:TP], src[0:64, :], w=[KS])
        P.ld(q, CS[:, 0:TP], prm["c_cos"])
        P.tt(dst[:, 0:TP], dst[:, 0:TP], CS[:, 0:TP], ALU.mult)
        P.ld(q, CS[:, 0:TP], prm["c_sin"])
        P.tt(KS[:, 0:TP], KS[:, 0:TP], CS[:, 0:TP], ALU.mult)
        P.tt(dst[:, 0:TP], dst[:, 0:TP], KS[:, 0:TP], ALU.add)
    P.act(K[:, 0:TP], K[:, 0:TP], AF.Copy, scale=float(128 ** -0.5))
    P.memset(SB3[:, NT - 1, :], 0.0, w=[SBALL])
    nv = 0
    for i in range(NT - 1, 0, -1):
        j = i % 2
        vti = ldv(i, nv); nv += 1
        P.tr(psb[j][:, 0:128], K[:, i * 128:(i + 1) * 128], ident[:])
        P.act(kbs[j][:], psb[j][:, 0:128], AF.Copy, scale=cv[:, 1:2])
        P.mm(psb[2 + j][:, 0:128], kbs[j][:], vti[:])
        P.stt(SB3[:, i - 1, :], SB3[:, i, :], cv[:, 2:3], psb[2 + j][:, 0:128], ALU.mult, ALU.add,
              r=[("sball", i), psb[2 + j], cv], w=[("sball", i - 1)])
    P.memset(sf[0][:], 0.0)
    for i in range(NT):
        j = i % 2
        sl = slice(i * 128, (i + 1) * 128)
        vti = ldv(i, nv); nv += 1
        P.mm(psb[4 + j][:, 0:128], K[:, sl], Q[:, sl])
        P.tt(pt_[j][:], psb[4 + j][:, 0:128], cm[:, 0, :], ALU.mult)
        P.tt(qf[j][:], Q[:, sl], cm[:, 1, :], ALU.mult, eng="pool")
        P.tt(qb[j][:], Q[:, sl], cm[:, 2, :], ALU.mult, eng="pool")
        P.mm(psb[6 + j][:, 0:128], pt_[j][:], vti[:], start=True, stop=False)
        P.mm(psb[6 + j][:, 0:128], qf[j][:], sf[j][:], start=False, stop=False)
        P.mm(psb[6 + j][:, 0:128], qb[j][:], SB3[:, i, :], start=False, stop=True, r=[qb[j], ("sball", i), SBALL], w=[psb[6 + j]])
        if i < NT - 1:
            P.tr(psb[j][:, 0:128], K[:, sl], ident[:])
            P.act(kbs[j][:], psb[j][:, 0:128], AF.Copy, scale=cv[:, 0:1])
            P.mm(psb[2 + j][:, 0:128], kbs[j][:], vti[:])
            P.stt(sf[1 - j][:], sf[j][:], cv[:, 2:3], psb[2 + j][:, 0:128], ALU.mult, ALU.add)
        if i == 0:
            P.ld("act", gt[j][PADL:128, :], pt[1][0:16, :])
        else:
            P.ld("act", gt[j][:], pt[1][16 + (i - 1) * 128:16 + i * 128, :])
        P.act(gt[j][:], gt[j][:], AF.Silu)
        P.memset(st4[j][:], 0.0)
        P.act(osb[j][:], psb[6 + j][:, 0:128], AF.Identity, accum_out=st4[j][:, 0:1])
        P.ts(st4[j][:, 1:2], st4[j][:, 0:1], -1.0 / 128, None, ALU.mult)
        P.act(junk[:], osb[j][:], AF.Square, bias=st4[j][:, 1:2], accum_out=st4[j][:, 2:3])
        P.act(st4[j][:, 3:4], st4[j][:, 2:3], AF.Sqrt, bias=1e-6, scale=1.0 / 128)
        P.recip(st4[j][:, 3:4], st4[j][:, 3:4])
        P.ts(osb[j][:], osb[j][:], st4[j][:, 1:2], st4[j][:, 3:4], ALU.add, ALU.mult)
        P.tt(osb[j][:], osb[j][:], gt[j][:], ALU.mult)
        if i == 0:
            P.ld(q, zout[0:16, :], osb[j][PADL:128, :])
        else:
            P.ld(q, zout[16 + (i - 1) * 128:16 + i * 128, :], osb[j][:])


def prep_C(g):
    d = {}
    gam = 1.0 - 2.0 ** (-5.0 - g)
    r = np.arange(128, dtype=np.float64)
    m = gam ** np.abs(r[:, None] - r[None, :])
    g1 = np.broadcast_to(gam ** (r + 1)[None, :], (128, 128))
    g2 = np.broadcast_to(gam ** (128 - r)[None, :], (128, 128))
    d["c_m"] = np.ascontiguousarray(np.stack([m, g1, g2], axis=1)).astype(np.float32)
    d["c_vec"] = np.stack([gam ** (127 - r), gam ** r, np.full(128, gam ** 128), np.zeros(128)], axis=1).astype(np.float32)
    d["ident"] = np.eye(128, dtype=np.float32)
    inv = (10000.0 ** (-np.arange(0, 128, 2, dtype=np.float32) / 128)).astype(np.float32)
    ang = (np.arange(T, dtype=np.float32)[:, None] * inv[None, :]).astype(np.float32)
    cos = np.cos(ang.astype(np.float64)).T; sin = np.sin(ang.astype(np.float64)).T
    c2 = np.zeros((128, TP), np.float32); s2 = np.zeros((128, TP), np.float32)
    c2[0:64, PADL:] = cos; c2[64:, PADL:] = cos
    s2[0:64, PADL:] = -sin; s2[64:, PADL:] = sin
    d["c_cos"] = c2; d["c_sin"] = s2
    return d


def bcast_last(ap2, idx, nblk=4, blk=32):
    (ps, pn), (fs, fn) = ap2.ap
    return bass.AP(ap2.tensor, ap2.offset + idx * fs, [[ps, pn], [blk * fs, nblk], [0, blk]])


def blk_view(ap2, blk=32):
    return ap2.rearrange("p (c r) -> p c r", r=blk)


def mixer_D(P, nc, st, big, psb, sm, pc, pt, prm, zout, layer, q="sync"):
    Q, KK, TMP, BC, OACC = big
    RM = sm["rmb"]
    pl = Pool128(sm["p128"])
    sb = lambda name, shape, dt=F32: st.enter_context(nc.sbuf_tensor("s_" + name, shape, dt))
    ident = pl.get(); msk = sm["p512"][0][:, 0:256].rearrange("p (a b) -> p a b", a=2); cmask = sb("d_cm", [128, 4])
    lbl = sb("d_lbl", [128, 2]); lb = sb("d_lb", [128, 2])
    NR = 2
    d1 = pl.get(NR)
    e1 = pl.get(NR)
    e2 = pl.get(NR)
    e3 = pl.get(NR)
    e4 = pl.get(NR)
    qm = pl.get(NR)
    km = pl.get(NR)
    kh = pl.get(NR)
    QM = [sm["p512"][1 + i][:].rearrange("p (a b) -> p a b", a=4) for i in range(NR)]
    khT = [sm["p512"][3 + i][:].rearrange("p (a b) -> p a b", a=4) for i in range(NR)]
    ptm = pl.get(NR); vt = pl.get(3); gt = pl.get(2); osb = pl.get(2); junk = pl.get()
    st4 = [sb("d_st%d" % i, [128, 2]) for i in range(2)]
    NS = 8
    S = pl.get(NS)
    P.ld(q, ident[:], prm["ident"]); P.ld(q, msk, prm["d_msk"]); P.ld(q, cmask[:], prm["d_cm"]); P.ld(q, lbl[:], prm["d_lbl"])
    if layer == 0:
        P.memset(lb[:, 0:1], 0.0); P.memset(lb[:, 1:2], 1.0)
    else:
        P.tt(lb[:, 0:1], lbl[:, 1:2], lbl[:, 0:1], ALU.subtract)
        P.act(lb[:, 0:1], lb[:, 0:1], AF.Sigmoid)
        P.ts(lb[:, 1:2], lb[:, 0:1], -1.0, 1.0, ALU.mult, ALU.add)
    for i in range(NR):
        P.memset(QM[i], 0.0)
    OA3 = OACC[:, 0:NT * 128].rearrange("p (n d) -> p n d", d=128)
    P.memset(Q[:, 0:PADL], 0.0)
    P.ld(q, Q[:, PADL:TP], pc[9])
    P.act(Q[:, PADL:TP], Q[:, PADL:TP], AF.Silu)
    for ph in range(2):
        bwd = (ph == 0)
        P.memset(KK[:, 0:PADL], 0.0); P.memset(TMP[:, 0:PADL], 0.0)
        P.ld(q, TMP[:, PADL:TP], pc[11 if bwd else 10])
        P.act(TMP[:, PADL:TP], TMP[:, PADL:TP], AF.Sigmoid)
        P.ts(TMP[:, PADL:TP], TMP[:, PADL:TP], lb[:, 1:2], lb[:, 0:1], ALU.mult, ALU.add)
        P.ts(KK[:, PADL:TP], TMP[:, PADL:TP], -1.0, 1.0, ALU.mult, ALU.add)
        P.act(TMP[:, PADL:TP], TMP[:, PADL:TP], AF.Ln)
        P.memset(RM[:, 0:TP], 1.0)
        rm3 = RM[:, 0:TP].rearrange("p (c r) -> p c r", r=32)
        if bwd:
            P.memset(rm3[:, :, 31:32], 0.0)
            P.scan(rev_ap(BC[:, 0:TP]), rev_ap(RM[:, 0:TP]), rev_ap(TMP[:, 0:TP]))
        else:
            P.memset(rm3[:, :, 0:1], 0.0)
            P.scan(BC[:, 0:TP], RM[:, 0:TP], TMP[:, 0:TP])
        ilast = 0 if bwd else 31
        cur = 0
        P.memset(S[0][:], 0.0)
        tiles = range(NT - 1, -1, -1) if bwd else range(NT)
        tiles = list(tiles)
        for n_, t in enumerate(tiles):
            j = n_ % NR
            sl = slice(t * 128, (t + 1) * 128)
            bct = BC[:, sl]
            vj = n_ % 3
            if t == 0:
                P.memset(vt[vj][:], 0.0)
                P.ld("act", vt[vj][PADL:128, :], pt[2][0:16, :])
            else:
                P.ld("act", vt[vj][:], pt[2][16 + (t - 1) * 128:16 + t * 128, :])
            P.tt(blk_view(d1[j][:]), blk_view(bct), bcast_last(bct, 15), ALU.subtract)
            P.act(e1[j][:], d1[j][:], AF.Exp)
            P.act(e2[j][:], d1[j][:], AF.Exp, scale=-1.0)
            P.tt(qm[j][:], Q[:, sl], e1[j][:], ALU.mult, eng="pool")
            P.tt(km[j][:], KK[:, sl], e2[j][:], ALU.mult, eng="pool")
            P.act(e3[j][:], bct, AF.Exp)
            P.tt(blk_view(e4[j][:]), bcast_last(bct, ilast), blk_view(bct), ALU.subtract)
            P.act(e4[j][:], e4[j][:], AF.Exp)
            P.tt(kh[j][:], KK[:, sl], e4[j][:], ALU.mult)
            qmd = bass.AP(QM[j].tensor, QM[j].offset, [[QM[j].ap[0][0], 128], [160, 4], [1, 32]])
            P.tt(qmd, blk_view(Q[:, sl]), blk_view(e3[j][:]), ALU.mult, w=[QM[j]])
            P.mm(psb[j][:, 0:128], km[j][:], qm[j][:])
            P.tt(ptm[j][:], psb[j][:, 0:128], msk[:, 1 if bwd else 0, :], ALU.mult)
            P.tr(psb[2 + j][:, 0:128], kh[j][:], ident[:])
            for c in range(4):
                P.act(khT[j][:, c, :], psb[2 + j][:, 0:128], AF.Copy, scale=cmask[:, c:c + 1])
            for c in range(4):
                P.mm(psb[4 + j][:, c * 128:(c + 1) * 128], khT[j][:, c, :], vt[vj][:])
            P.mm(psb[6 + j][:, 0:128], ptm[j][:], vt[vj][:], start=True, stop=False)
            corder = range(3, -1, -1) if bwd else range(4)
            for ci, c in enumerate(corder):
                P.mm(psb[6 + j][:, 0:128], QM[j][:, c, :], S[cur][:], start=False, stop=(ci == 3))
                nxt = (cur + 1) % NS
                P.stt(S[nxt][:], S[cur][:], e3[j][:, c * 32 + ilast:c * 32 + ilast + 1], psb[4 + j][:, c * 128:(c + 1) * 128], ALU.mult, ALU.add)
                cur = nxt
            if bwd:
                P.cp(OA3[:, t, :], psb[6 + j][:, 0:128], eng="act", w=[("oacc", t)])
            else:
                oj = n_ % 2
                if t == 0:
                    P.ld("act", gt[oj][PADL:128, :], pt[3][0:16, :])
                else:
                    P.ld("act", gt[oj][:], pt[3][16 + (t - 1) * 128:16 + t * 128, :])
                P.act(gt[oj][:], gt[oj][:], AF.Silu)
                P.tt(osb[oj][:], psb[6 + j][:, 0:128], OA3[:, t, :], ALU.add, r=[psb[6 + j], ("oacc", t)], w=[osb[oj]])
                P.memset(st4[oj][:], 0.0)
                P.act(junk[:], osb[oj][:], AF.Square, accum_out=st4[oj][:, 0:1])
                P.act(st4[oj][:, 1:2], st4[oj][:, 0:1], AF.Sqrt, bias=1e-6, scale=1.0 / 128)
                P.recip(st4[oj][:, 1:2], st4[oj][:, 1:2])
                P.stt(osb[oj][:], osb[oj][:], st4[oj][:, 1:2], gt[oj][:], ALU.mult, ALU.mult)
                if t == 0:
                    P.ld(q, zout[0:16, :], osb[oj][PADL:128, :])
                else:
                    P.ld(q, zout[16 + (t - 1) * 128:16 + t * 128, :], osb[oj][:])


def prep_D(inp, l, g):
    d = {}
    sl = slice(g * 128, (g + 1) * 128)
    d["ident"] = np.eye(128, dtype=np.float32)
    r = np.arange(128)
    same = (r[:, None] // 32) == (r[None, :] // 32)
    mf = same & (r[:, None] <= r[None, :])
    mb = same & (r[:, None] >= r[None, :])
    d["d_msk"] = np.ascontiguousarray(np.stack([mf, mb], axis=1)).astype(np.float32)
    d["d_cm"] = ((r[:, None] // 32) == np.arange(4)[None, :]).astype(np.float32)
    d["d_lbl"] = np.ascontiguousarray(inp["hgrn_lb_logits"][:, sl].T)
    return d


NFFT = 16384
NJ = 16416
TWO_PI = float(2 * np.pi)


def mixer_B(P, nc, st, big, psb, sm, pc, prm, zout, q="sync"):
    B0, B1, B2, X0C, B4 = big
    sb = lambda name, shape, dt=F32: st.enter_context(nc.sbuf_tensor("s_" + name, shape, dt))
    pl = Pool128(sm["p128"])
    UD = nc.dram_tensor("b_UD", [128, NFFT], F32).ap()
    KD = nc.dram_tensor("b_KD", [128, NJ], F32).ap()
    YD = nc.dram_tensor("b_YD", [128, NFFT], F32).ap()
    cw = sb("b_cw", [128, 9]); cb = sb("b_cb", [128, 3]); bias = sb("b_bias", [128, 1]); nad = sb("b_nad", [128, 2])
    b12 = sb("b_b12", [64, 2]); negpi = sb("b_negpi", [128, 1])
    hbx = sb("b_hbx", [128, 32])
    w3 = sm["p512"][0][0:64, 0:256].rearrange("p (a b) -> p a b", a=2)
    F1 = sm["p512"][1][:, 0:256]
    IF1 = sm["p512"][2][:].rearrange("p (a b) -> p a b", a=2)
    TW = sm["p512"][3][:, 0:256].rearrange("p (a b) -> p a b", a=2)
    ICN = sm["p512"][4][:, 0:256].rearrange("p (a b) -> p a b", a=2)
    FS = pl.get(); ones1 = pl.get()[0:1, :]
    ztile = pl.get(2); tpos = pl.get(2); h1 = pl.get(2); h2 = pl.get(2); ets = pl.get(2)
    w1 = pl.get()[0:33, 0:64]; w2 = pl.get()[0:64, 0:64]
    rm32 = sm["rmb"][:].bitcast(F32)
    tmpL = [rm32[:, i * 512:(i + 1) * 512] for i in range(8)]
    P.ld(q, cw[:], prm["b_cw"]); P.ld(q, cb[:], prm["b_cb"]); P.ld(q, bias[:], prm["b_bias"]); P.ld(q, nad[:], prm["b_decay"])
    P.ld(q, w1, prm["b_w1"]); P.ld(q, w2, prm["b_w2"]); P.ld(q, b12[:], prm["b_b12"])
    P.ld(q, F1, prm["b_F1"]); P.ld(q, IF1, prm["b_IF1"]); P.ld(q, TW, prm["b_TW"]); P.ld(q, ICN, prm["b_ICN"]); P.ld(q, FS[:], prm["b_FS"])
    w3t = w3; P.ld(q, w3t, prm["b_w3"])
    P.memset(negpi[:], -float(np.pi)); P.memset(ones1, 1.0)
    P.act(nad[:], nad[:], AF.Abs)
    P.ts(nad[:], nad[:], -1.0, None, ALU.mult)
    bhq = sb("b_bhq", [64, 4])
    P.ts(bhq[:, 0:2], b12[:], 0.5, None, ALU.mult)
    P.ts(bhq[:, 2:4], b12[:], 0.25, None, ALU.mult)
    s4 = pl.get(2)
    outs = [X0C, B4, B0]
    for i in range(3):
        src = B1 if i < 2 else B2
        P.memset(src[:, 0:1], 0.0); P.memset(src[:, T + 1:T + 2], 0.0)
        P.ld(q, src[:, 1:1 + T], pc[2 + i])
        o = outs[i]
        P.act(o[:, 0:T], src[:, 1:1 + T], AF.Identity, bias=cb[:, i:i + 1], scale=cw[:, 3 * i + 1:3 * i + 2])
        P.stt(o[:, 0:T], src[:, 0:T], cw[:, 3 * i:3 * i + 1], o[:, 0:T], ALU.mult, ALU.add)
        P.stt(o[:, 0:T], src[:, 2:2 + T], cw[:, 3 * i + 2:3 * i + 3], o[:, 0:T], ALU.mult, ALU.add)
    P.tt(B4[:, 0:T], B4[:, 0:T], B0[:, 0:T], ALU.mult)
    P.memset(B1[:, 0:NFFT - T], 0.0)
    P.ld(q, UD[:, 0:T], B4[:, 0:T]); P.ld(q, UD[:, T:NFFT], B1[:, 0:NFFT - T])
    segs = []
    for lo, hi, d in ((0, T, 0), (T, NJ, 1)):
        j0 = lo
        while j0 < hi:
            n = min(128, hi - j0); segs.append((j0, n, d)); j0 += n
    for si, (j0, n, d) in enumerate(segs):
        j = si % 2
        P.ld(q, ztile[j][0:33, 0:n], prm["b_z"][:, j0:j0 + n]); P.ld("act", tpos[j][0:1, 0:n], prm["b_tpos"][:, j0:j0 + n])
        P.mm(psb[j][0:64, 0:n], w1, ztile[j][0:33, 0:n])
        P.act(h1[j][0:64, 0:n], psb[j][0:64, 0:n], AF.Sin, bias=bhq[:, 0:1], scale=0.5)
        P.act(s4[j][0:64, 0:n], psb[j][0:64, 0:n], AF.Sin, bias=bhq[:, 2:3], scale=0.25)
        P.tt(s4[j][0:64, 0:n], s4[j][0:64, 0:n], s4[j][0:64, 0:n], ALU.mult)
        P.ts(s4[j][0:64, 0:n], s4[j][0:64, 0:n], -2.0, 1.0, ALU.mult, ALU.add)
        P.stt(h1[j][0:64, 0:n], h1[j][0:64, 0:n], 2.0, s4[j][0:64, 0:n], ALU.mult, ALU.mult)
        P.mm(psb[2 + j][0:64, 0:n], w2, h1[j][0:64, 0:n])
        P.act(h2[j][0:64, 0:n], psb[2 + j][0:64, 0:n], AF.Sin, bias=bhq[:, 1:2], scale=0.5)
        P.act(s4[j][0:64, 0:n], psb[2 + j][0:64, 0:n], AF.Sin, bias=bhq[:, 3:4], scale=0.25)
        P.tt(s4[j][0:64, 0:n], s4[j][0:64, 0:n], s4[j][0:64, 0:n], ALU.mult)
        P.ts(s4[j][0:64, 0:n], s4[j][0:64, 0:n], -2.0, 1.0, ALU.mult, ALU.add)
        P.stt(h2[j][0:64, 0:n], h2[j][0:64, 0:n], 2.0, s4[j][0:64, 0:n], ALU.mult, ALU.mult)
        P.mm(psb[4 + j][:, 0:n], w3t[:, d, :], h2[j][0:64, 0:n])
        P.mm(psb[6 + j][:, 0:n], ones1, tpos[j][0:1, 0:n])
        et = ets[j]
        P.act(et[:, 0:n], psb[6 + j][:, 0:n], AF.Exp, scale=nad[:, d:d + 1])
        P.tt(et[:, 0:n], et[:, 0:n], psb[4 + j][:, 0:n], ALU.mult)
        P.ld(q, KD[:, j0:j0 + n], et[:, 0:n])
    P.ld(q, hbx[:, 0:32], KD[:, NFFT:NJ])

    def twiddle(ps, dre, dim_, inverse, kre, kim, jj):
        p3 = ps[:, 0:512].rearrange("p (c x) -> p c x", c=2) if hasattr(ps, "name") else ps.rearrange("p (c x) -> p c x", c=2)
        pre, pim = p3[:, :, 0:128], p3[:, :, 128:256]
        (ps_, pn_), _, (fs_, fn_) = TW.ap
        twc = bass.AP(TW.tensor, TW.offset, [[ps_, pn_], [0, 2], [fs_, 128]])
        tws = bass.AP(TW.tensor, TW.offset + 128 * fs_, [[ps_, pn_], [0, 2], [fs_, 128]])
        t = [sm_t.rearrange("p (c x) -> p c x", c=2) for sm_t in (tmpL[jj][:, 0:256], tmpL[jj][:, 256:512], tmpL[2 + jj][:, 0:256], tmpL[2 + jj][:, 256:512])]
        ps = psb[jj]
        P.tt(t[0], pre, twc, ALU.mult, r=[ps, TW], w=[("tw", jj, 0)])
        P.tt(t[1], pim, tws, ALU.mult, r=[ps, TW], w=[("tw", jj, 1)])
        P.tt(t[2], pre, tws, ALU.mult, r=[ps, TW], w=[("tw", jj, 2)])
        P.tt(t[3], pim, twc, ALU.mult, r=[ps, TW], w=[("tw", jj, 3)])
        if not inverse:
            P.tt(dre, t[0], t[1], ALU.subtract, eng="pool", r=[("tw", jj, 0), ("tw", jj, 1)], w=[kre])
            P.tt(dim_, t[2], t[3], ALU.add, eng="pool", r=[("tw", jj, 2), ("tw", jj, 3)], w=[kim])
        else:
            P.tt(dre, t[0], t[1], ALU.add, eng="pool", r=[("tw", jj, 0), ("tw", jj, 1)], w=[kre])
            P.tt(dim_, t[3], t[2], ALU.subtract, eng="pool", r=[("tw", jj, 2), ("tw", jj, 3)], w=[kim])

    Cm = F1[:, 0:128]; mS = F1[:, 128:256]

    def fwd(INv, GRE, GIM, nch, kin, kgre, kgim, post):
        for c2 in range(nch // 2):
            jj = c2 % 2
            for cc in range(2):
                ch = 2 * c2 + cc
                P.mm(psb[jj][:, cc * 256:(cc + 1) * 256], INv[:, ch, :], F1, r=[kin, F1], w=[psb[jj]])
            twiddle(psb[jj][:, 0:512], GRE[:, 2 * c2:2 * c2 + 2, :], GIM[:, 2 * c2:2 * c2 + 2, :], False, kgre, kgim, jj)
        GREf = GRE.rearrange("p c x -> p (c x)"); GIMf = GIM.rearrange("p c x -> p (c x)")
        for ti in range(nch * 128 // 512):
            jj = ti % 2
            sl = slice(ti * 512, (ti + 1) * 512)
            P.mm(psb[4 + jj][:, :], Cm, GREf[:, sl], start=True, stop=False, r=[F1, kgre], w=[psb[4 + jj]])
            P.mm(psb[4 + jj][:, :], FS[:], GIMf[:, sl], start=False, stop=True, r=[FS, kgim], w=[psb[4 + jj]])
            P.mm(psb[6 + jj][:, :], Cm, GIMf[:, sl], start=True, stop=False, r=[F1, kgim], w=[psb[6 + jj]])
            P.mm(psb[6 + jj][:, :], mS, GREf[:, sl], start=False, stop=True, r=[F1, kgre], w=[psb[6 + jj]])
            post(ti, psb[4 + jj], psb[6 + jj], jj)

    for hc in range(2):
        ch0 = 64 * hc
        P.barrier()
        IN = B0[:, 0:8192].rearrange("p (c x) -> p c x", x=128)
        P.ld(q, IN, KD[ch0:ch0 + 64, 0:NFFT].rearrange("c (a b) -> a c b", b=128), w=["kin"])
        GRE = B1[:, 0:8192].rearrange("p (c x) -> p c x", x=128); GIM = B2[:, 0:8192].rearrange("p (c x) -> p c x", x=128)

        def post_k(ti, pr, pi, jj):
            sl = slice(ti * 512, (ti + 1) * 512)
            P.cp(B0[:, sl], pr[:, :], eng="act", r=[pr, "kin"], w=[("kre", ti)])
            P.cp(B4[:, sl], pi[:, :], eng="dve", r=[pi], w=[("kim", ti)])
        fwd(IN, GRE, GIM, 64, "kin", "kgre", "kgim", post_k)
        for qq in range(2):
            c0 = ch0 + 32 * qq
            P.barrier()
            UIN = B1[:, 0:4096].rearrange("p (c x) -> p c x", x=128)
            UGRE = B1[:, 4096:8192].rearrange("p (c x) -> p c x", x=128)
            UGIM = B2[:, 0:4096].rearrange("p (c x) -> p c x", x=128)
            YIM = B2[:, 4096:8192]
            P.ld(q, UIN, UD[c0:c0 + 32, :].rearrange("c (a b) -> a c b", b=128), r=[UD, "kgre", "kgim"], w=["uin"])

            def post_u(ti, pr, pi, jj, qq=qq):
                sl = slice(ti * 512, (ti + 1) * 512)
                ks = slice(qq * 4096 + ti * 512, qq * 4096 + (ti + 1) * 512)
                kt = (qq * 4096) // 512 + ti
                t = [tmpL[4 + jj], tmpL[6 + jj]]
                P.tt(t[0], pr[:, :], B0[:, ks], ALU.mult, r=[pr, ("kre", kt)], w=[("pu", jj, 0)])
                P.tt(t[1], pi[:, :], B4[:, ks], ALU.mult, r=[pi, ("kim", kt)], w=[("pu", jj, 1)])
                P.tt(B1[:, sl], t[0], t[1], ALU.subtract, eng="pool", r=[("pu", jj, 0), ("pu", jj, 1), "uin"], w=[("yre", ti)])
                P.tt(t[0], pr[:, :], B4[:, ks], ALU.mult, r=[pr, ("kim", kt), ("pu", jj, 0)], w=[("pu", jj, 0)])
                P.tt(t[1], pi[:, :], B0[:, ks], ALU.mult, r=[pi, ("kre", kt), ("pu", jj, 1)], w=[("pu", jj, 1)])
                P.tt(YIM[:, sl], t[0], t[1], ALU.add, eng="pool", r=[("pu", jj, 0), ("pu", jj, 1)], w=[("yim", ti)])
            fwd(UIN, UGRE, UGIM, 32, "uin", "ugre", "ugim", post_u)
            YRE3 = B1[:, 0:4096].rearrange("p (c x) -> p c x", x=128); YIM3 = YIM.rearrange("p (c x) -> p c x", x=128)
            yre_keys = [("yre", ti) for ti in range(8)]; yim_keys = [("yim", ti) for ti in range(8)]
            for c2 in range(16):
                jj = c2 % 2
                for cc in range(2):
                    ch = 2 * c2 + cc
                    P.mm(psb[jj][:, cc * 256:(cc + 1) * 256], YRE3[:, ch, :], IF1[:, 0, :], start=True, stop=False, r=yre_keys + [IF1], w=[psb[jj]])
                    P.mm(psb[jj][:, cc * 256:(cc + 1) * 256], YIM3[:, ch, :], IF1[:, 1, :], start=False, stop=True, r=yim_keys + [IF1], w=[psb[jj]])
                twiddle(psb[jj][:, 0:512], UGRE[:, 2 * c2:2 * c2 + 2, :], UGIM[:, 2 * c2:2 * c2 + 2, :], True, "ugre", "ugim", jj)
            HRE = B1[:, 4096:8192]; HIM = B2[:, 0:4096]
            for ti in range(8):
                jj = ti % 2
                sl = slice(ti * 512, (ti + 1) * 512)
                P.mm(psb[4 + jj][0:65, :], ICN[:, 0, 0:65], HRE[:, sl], start=True, stop=False, r=[ICN, "ugre"], w=[psb[4 + jj]])
                P.mm(psb[4 + jj][0:65, :], ICN[:, 1, 0:65], HIM[:, sl], start=False, stop=True, r=[ICN, "ugim"], w=[psb[4 + jj]])
                P.cp(tmpL[4 + jj][0:65, :], psb[4 + jj][0:65, :], eng="act", r=[psb[4 + jj]], w=[("pu", jj, 0)])
                P.ld(q, YD[c0 + 4 * ti:c0 + 4 * ti + 4, 0:65 * 128].rearrange("c (a b) -> a c b", b=128),
                     tmpL[4 + jj][0:65, :].rearrange("p (c x) -> p c x", x=128), r=[("pu", jj, 0)], w=[YD])
    P.barrier()
    P.ld(q, B0[:, 0:T], YD[:, 0:T], r=[YD], w=[B0])
    P.ld(q, B1[:, 0:T], UD[:, 0:T], r=[UD], w=[B1])
    for m in range(31):
        P.stt(B0[:, 0:31 - m], B1[:, 8177 + m:T], hbx[:, m:m + 1], B0[:, 0:31 - m], ALU.mult, ALU.add)
    P.stt(B0[:, 0:T], B1[:, 0:T], bias[:, 0:1], B0[:, 0:T], ALU.mult, ALU.add)
    P.tt(B0[:, 0:T], B0[:, 0:T], X0C[:, 0:T], ALU.mult)
    P.ld(q, zout, B0[:, 0:T])


def prep_B(inp, l, g):
    d = {}
    sl = slice(g * 128, (g + 1) * 128)
    cw = np.zeros((128, 9), np.float32); cb = np.zeros((128, 3), np.float32)
    for i in range(3):
        cols = slice(512 * i + g * 128, 512 * i + (g + 1) * 128)
        cw[:, 3 * i:3 * i + 3] = inp["hy_conv_w"][l][:, cols].T
        cb[:, i] = inp["hy_conv_b"][l][cols]
    d["b_cw"] = cw; d["b_cb"] = cb
    d["b_bias"] = np.ascontiguousarray(inp["hy_bias"][l][sl][:, None])
    d["b_decay"] = np.ascontiguousarray(np.stack([inp["hy_decay"][l][0:512][sl], inp["hy_decay"][l][512:1024][sl]], axis=1))
    d["b_w1"] = np.ascontiguousarray(inp["hy_w1"][l]); d["b_w2"] = np.ascontiguousarray(inp["hy_w2"][l])
    d["b_b12"] = np.ascontiguousarray(np.stack([inp["hy_b1"][l], inp["hy_b2"][l]], axis=1))
    d["b_w3"] = np.ascontiguousarray(np.stack([inp["hy_w3"][l][:, 0:512][:, sl], inp["hy_w3"][l][:, 512:1024][:, sl]], axis=1))
    pos = np.zeros(NJ, np.int64)
    pos[0:T] = np.arange(T); pos[T:NFFT] = NFFT - np.arange(T, NFFT); pos[NFFT:NFFT + 31] = 8177 + np.arange(31)
    n = pos.astype(np.float32)
    t = (n / np.float32(T - 1)).astype(np.float32)
    freqs = np.linspace(1e-4, 15, 16, dtype=np.float32)
    ang = (np.float32(2.0 * np.pi) * n[:, None] * freqs[None, :] / np.float32(T)).astype(np.float32)
    z = np.concatenate([t[:, None], np.cos(ang), -np.sin(ang)], axis=1).astype(np.float32)
    d["b_z"] = np.ascontiguousarray(z.T); d["b_tpos"] = np.ascontiguousarray(t[None, :])
    a = np.arange(128, dtype=np.float64)
    th = 2 * np.pi * np.outer(a, a) / 128
    C, S = np.cos(th), np.sin(th)
    d["b_F1"] = np.concatenate([C, -S], axis=1).astype(np.float32)
    d["b_FS"] = S.astype(np.float32)
    d["b_IF1"] = np.ascontiguousarray(np.stack([np.concatenate([C, S], axis=1), np.concatenate([-S, C], axis=1)], axis=1)).astype(np.float32)
    tw = 2 * np.pi * np.outer(a, a) / NFFT
    d["b_TW"] = np.ascontiguousarray(np.stack([np.cos(tw), -np.sin(tw)], axis=1)).astype(np.float32)
    d["b_ICN"] = np.ascontiguousarray(np.stack([C / NFFT, -S / NFFT], axis=1)).astype(np.float32)
    return d


D = 2048
KC = 16
NTOK = 2052
HALF = 1026
NW = 342
DFF = 5632
JC = 44
MIXC = 7168


def _rmsnorm(P, nc, env, h_dram, gain_sb, nT, half, fin=None):
    hs, sq, ones, PSS, rstd = env["hs"], env["sq"], env["ones"], env["PSA"], env["rstd"]
    c0 = half * HALF
    for k in range(KC):
        j = k % 2
        P.ld("sync", hs[j][:], h_dram[:, k, c0:c0 + HALF])
        P.act(sq[j][:], hs[j][:], AF.Square)
        for n in range(3):
            P.mm(PSS[:, n, 0:NW], ones[:], sq[j][:, n * NW:(n + 1) * NW], start=(k == 0), stop=(k == KC - 1))
    P.act(rstd[:].rearrange("p (n f) -> p n f", n=3), PSS[:, :, 0:NW], AF.Sqrt, bias=env["eps"][:, 0:1], scale=1.0 / D)
    P.recip(rstd[:], rstd[:])
    for k in range(KC):
        j = k % 2
        P.ld("sync", hs[j][:], h_dram[:, k, c0:c0 + HALF])
        if fin is None:
            P.stt(nT[:, k, :], hs[j][:], gain_sb[:, k:k + 1], rstd[:], ALU.mult, ALU.mult, w=[("nT", k)])
        else:
            P.stt(env["ot"][j][:], hs[j][:], gain_sb[:, k:k + 1], rstd[:], ALU.mult, ALU.mult)
            P.ld("sync", fin[:, k, c0:c0 + HALF], env["ot"][j][:])


def _wload(P, wt, w2d, nk, eng="pool"):
    P.ld(eng, wt[:, 0:nk, :], w2d.rearrange("(k p) m -> p k m", p=128))


def _common(nc, st):
    sb = lambda name, shape, dt=F32: st.enter_context(nc.sbuf_tensor(name, shape, dt))
    env = {}
    env["hs"] = [sb("hs%d" % i, [128, HALF]) for i in range(2)]
    env["sq"] = [sb("sq%d" % i, [128, HALF]) for i in range(2)]
    env["ones"] = sb("ones", [128, 128])
    env["rstd"] = sb("rstd", [128, HALF])
    env["eps"] = sb("eps", [128, 1])
    env["PSA"] = st.enter_context(nc.psum_tensor("PSA", [128, 3, 512], F32))
    env["PSB"] = st.enter_context(nc.psum_tensor("PSB", [128, 3, 512], F32))
    env["nT"] = sb("nT", [128, KC, HALF], BF16)
    env["wt"] = [sb("wt%d" % i, [128, JC, 128], BF16) for i in range(2)]
    env["ot"] = [sb("ot%d" % i, [128, HALF]) for i in range(2)]
    return env, sb


def build_p1(nc):
    hT = nc.dram_tensor("hT", [128, KC, NTOK], F32, kind="ExternalInput").ap()
    gain = nc.dram_tensor("gain", [128, KC], F32, kind="ExternalInput").ap()
    w = nc.dram_tensor("w", [D, MIXC], F32, kind="ExternalInput").ap()
    pm = nc.dram_tensor("pm", [MIXC // 128, 128, NTOK], F32, kind="ExternalOutput").ap()
    with contextlib.ExitStack() as st:
        env, sb = _common(nc, st)
        gs = sb("gs", [128, KC])
        P = PX(nc)
        P.memset(env["ones"][:], 1.0); P.memset(env["eps"][:], 1e-6)
        P.ld("sync", gs[:], gain)
        nT = env["nT"]
        job = 0
        for half in range(2):
            _rmsnorm(P, nc, env, hT, gs, nT, half)
            for m in range(MIXC // 128):
                j = job % 2; job += 1
                ps = env["PSA"] if j == 0 else env["PSB"]
                _wload(P, env["wt"][j], w[:, m * 128:(m + 1) * 128], KC)
                for k in range(KC):
                    for n in range(3):
                        P.mm(ps[:, n, 0:NW], env["wt"][j][:, k, :], nT[:, k, n * NW:(n + 1) * NW], start=(k == 0), stop=(k == KC - 1),
                             r=[env["wt"][j], ("nT", k)])
                P.act(env["ot"][j][:].rearrange("p (n f) -> p n f", n=3), ps[:, :, 0:NW], AF.Copy)
                P.ld("sync", pm[m, :, half * HALF:(half + 1) * HALF], env["ot"][j][:])
        P.emit()
    return nc


def build_p3(nc, last):
    hT = nc.dram_tensor("hT", [128, KC, NTOK], F32, kind="ExternalInput").ap()
    zT = nc.dram_tensor("zT", [128, KC, NTOK], F32, kind="ExternalInput").ap()
    gains = nc.dram_tensor("gains", [128, 3, KC], F32, kind="ExternalInput").ap()
    wg = nc.dram_tensor("wg", [D, 4 * D], F32, kind="ExternalInput").ap()
    wbo = nc.dram_tensor("wbo", [4, 512, D], F32, kind="ExternalInput").ap()
    wout = nc.dram_tensor("wout", [D, D], F32, kind="ExternalInput").ap()
    wfg = nc.dram_tensor("wfg", [D, DFF], F32, kind="ExternalInput").ap()
    wfu = nc.dram_tensor("wfu", [D, DFF], F32, kind="ExternalInput").ap()
    wfd = nc.dram_tensor("wfd", [DFF, D], F32, kind="ExternalInput").ap()
    hmid = nc.dram_tensor("hmid", [128, KC, NTOK], F32).ap()
    hout = nc.dram_tensor("hout", [128, KC, NTOK], F32, kind="ExternalOutput").ap()
    if last:
        hfin = nc.dram_tensor("hfin", [128, KC, NTOK], F32, kind="ExternalOutput").ap()
    with contextlib.ExitStack() as st:
        env, sb = _common(nc, st)
        gs = sb("gs", [128, 3, KC])
        UN = sb("UN", [128, JC, HALF], BF16)
        gsig = sb("gsig", [128, HALF]); acc = sb("acc", [128, HALF]); prod = sb("prod", [128, HALF])
        P = PX(nc)
        P.memset(env["ones"][:], 1.0); P.memset(env["eps"][:], 1e-6)
        P.ld("sync", gs[:], gains)
        nT = env["nT"]; PSA = env["PSA"]; PSB = env["PSB"]; wt = env["wt"]; ot = env["ot"]; hs = env["hs"]
        v3 = lambda t: t[:].rearrange("p (n f) -> p n f", n=3)
        wj = 0
        for half in range(2):
            c0 = half * HALF
            _rmsnorm(P, nc, env, hT, gs[:, 0, :], nT, half)
            for k in range(KC):
                P.ld("pool", UN[:, k, :], zT[:, k, c0:c0 + HALF], w=[("un", k)])
            for m in range(KC):
                for b in range(4):
                    j = wj % 2; wj += 1
                    _wload(P, wt[j], wg[:, b * D + m * 128: b * D + (m + 1) * 128], KC)
                    for k in range(KC):
                        for n in range(3):
                            P.mm(PSA[:, n, 0:NW], wt[j][:, k, :], nT[:, k, n * NW:(n + 1) * NW], start=(k == 0), stop=(k == KC - 1),
                                 r=[wt[j], ("nT", k)])
                    P.act(v3(gsig), PSA[:, :, 0:NW], AF.Sigmoid)
                    j = wj % 2; wj += 1
                    _wload(P, wt[j], wbo[b, :, m * 128:(m + 1) * 128], 4)
                    for k in range(4):
                        for n in range(3):
                            P.mm(PSB[:, n, 0:NW], wt[j][:, k, :], UN[:, b * 4 + k, n * NW:(n + 1) * NW], start=(k == 0), stop=(k == 3),
                                 r=[wt[j], ("un", b * 4 + k)])
                    if b == 0:
                        P.tt(v3(acc), v3(gsig), PSB[:, :, 0:NW], ALU.mult)
                    else:
                        P.tt(v3(prod), v3(gsig), PSB[:, :, 0:NW], ALU.mult)
                        if b < 3:
                            P.tt(acc[:], acc[:], prod[:], ALU.add)
                        else:
                            P.tt(UN[:, 16 + m, :], acc[:], prod[:], ALU.add, w=[("un", 16 + m)])
            for m in range(KC):
                j = wj % 2; wj += 1
                _wload(P, wt[j], wout[:, m * 128:(m + 1) * 128], KC)
                for k in range(KC):
                    for n in range(3):
                        P.mm(PSA[:, n, 0:NW], wt[j][:, k, :], UN[:, 16 + k, n * NW:(n + 1) * NW], start=(k == 0), stop=(k == KC - 1),
                             r=[wt[j], ("un", 16 + k)])
                P.ld("sync", hs[m % 2][:], hT[:, m, c0:c0 + HALF])
                P.tt(v3(ot[m % 2]), v3(hs[m % 2]), PSA[:, :, 0:NW], ALU.add)
                P.ld("sync", hmid[:, m, c0:c0 + HALF], ot[m % 2][:])
            _rmsnorm(P, nc, env, hmid, gs[:, 1, :], nT, half)
            for jc in range(JC):
                j = wj % 2; wj += 1
                _wload(P, wt[j], wfg[:, jc * 128:(jc + 1) * 128], KC)
                for k in range(KC):
                    for n in range(3):
                        P.mm(PSA[:, n, 0:NW], wt[j][:, k, :], nT[:, k, n * NW:(n + 1) * NW], start=(k == 0), stop=(k == KC - 1),
                             r=[wt[j], ("nT", k)])
                P.act(v3(gsig), PSA[:, :, 0:NW], AF.Silu)
                j = wj % 2; wj += 1
                _wload(P, wt[j], wfu[:, jc * 128:(jc + 1) * 128], KC)
                for k in range(KC):
                    for n in range(3):
                        P.mm(PSB[:, n, 0:NW], wt[j][:, k, :], nT[:, k, n * NW:(n + 1) * NW], start=(k == 0), stop=(k == KC - 1),
                             r=[wt[j], ("nT", k)])
                P.tt(UN[:, jc, :].rearrange("p (n f) -> p n f", n=3), v3(gsig), PSB[:, :, 0:NW], ALU.mult,
                     r=[gsig, PSB] + [("un", 16 + k) for k in range(KC)] + ([("un", jc)] if jc < 32 else []), w=[("un", jc)])
            for m in range(KC):
                j = wj % 2; wj += 1
                _wload(P, wt[j], wfd[:, m * 128:(m + 1) * 128], JC)
                for k in range(JC):
                    for n in range(3):
                        P.mm(PSA[:, n, 0:NW], wt[j][:, k, :], UN[:, k, n * NW:(n + 1) * NW], start=(k == 0), stop=(k == JC - 1),
                             r=[wt[j], ("un", k)])
                P.ld("sync", hs[m % 2][:], hmid[:, m, c0:c0 + HALF])
                P.tt(v3(ot[m % 2]), v3(hs[m % 2]), PSA[:, :, 0:NW], ALU.add)
                P.ld("sync", hout[:, m, c0:c0 + HALF], ot[m % 2][:])
            if last:
                _rmsnorm(P, nc, env, hout, gs[:, 2, :], nT, half, fin=hfin)
        P.emit()
    return nc


def build_mixer_prog(layer):
    nc = bass.Bass("TRN2", target_bir_lowering=False)
    pc = nc.dram_tensor("pc", [14, 128, T], F32, kind="ExternalInput").ap()
    pt = nc.dram_tensor("pt", [4, T, 128], F32, kind="ExternalInput").ap()
    shapes = mixer_param_shapes()
    prm = {k: nc.dram_tensor(k, list(v), F32, kind="ExternalInput").ap() for k, v in shapes.items()}
    za = nc.dram_tensor("za", [128, T], F32, kind="ExternalOutput").ap()
    zb = nc.dram_tensor("zb", [128, T], F32, kind="ExternalOutput").ap()
    zc = nc.dram_tensor("zc", [T, 128], F32, kind="ExternalOutput").ap()
    zd = nc.dram_tensor("zd", [T, 128], F32, kind="ExternalOutput").ap()
    with contextlib.ExitStack() as st:
        big = alloc_big(nc, st); psb = alloc_psum(nc, st); sm = alloc_small(nc, st)
        P = PX(nc)
        mixer_A(P, nc, st, big, psb, sm, pc, prm, za)
        P.barrier()
        mixer_C(P, nc, st, big, psb, sm, pc, pt, prm, zc)
        P.barrier()
        mixer_D(P, nc, st, big, psb, sm, pc, pt, prm, zd, layer)
        P.barrier()
        mixer_B(P, nc, st, big, psb, sm, pc, prm, zb)
        P.emit()
    return nc


def mixer_param_shapes():
    return {k: v.shape for k, v in _mixer_params_example().items()}


_EX = {}


def _mixer_params_example():
    if not _EX:
        fake = dict(lru_conv_w=np.zeros((2, 4, 512), np.float32), lru_conv_b=np.zeros((2, 512), np.float32),
                    lru_wa=np.zeros((2, 2, 8, 64, 64), np.float32), lru_wx=np.zeros((2, 2, 8, 64, 64), np.float32),
                    lru_ba=np.zeros((2, 2, 512), np.float32), lru_bx=np.zeros((2, 2, 512), np.float32),
                    lru_lambda=np.zeros((2, 2, 512), np.float32), hgrn_lb_logits=np.zeros((2, 512), np.float32),
                    hy_conv_w=np.zeros((2, 3, 1536), np.float32), hy_conv_b=np.zeros((2, 1536), np.float32),
                    hy_w1=np.zeros((2, 33, 64), np.float32), hy_b1=np.zeros((2, 64), np.float32),
                    hy_w2=np.zeros((2, 64, 64), np.float32), hy_b2=np.zeros((2, 64), np.float32),
                    hy_w3=np.zeros((2, 64, 1024), np.float32), hy_decay=np.zeros((2, 1024), np.float32),
                    hy_bias=np.zeros((2, 512), np.float32))
        _EX.update(mixer_params(fake, 0, 0))
    return _EX


def mixer_params(inp, l, g):
    d = {}
    d.update(prep_A(inp, l, g)); d.update(prep_C(g)); d.update(prep_D(inp, l, g)); d.update(prep_B(inp, l, g))
    return d


def _run(nc, in_maps):
    res = run_bass_kernel_spmd(nc, in_maps, core_ids=list(range(8)))
    return res.results


def _to_fm(a2d):
    nt, nf = a2d.shape
    return np.ascontiguousarray(a2d.T.reshape(nf // 128, 128, nt).transpose(1, 0, 2))


def kernel(**inp):
    inp = {k: np.asarray(v) for k, v in inp.items()}
    x, meta = inp["x"], inp["meta"]
    B = x.shape[0]
    h = np.concatenate([np.broadcast_to(meta[None], (B, 16, D)), x], axis=1)
    hT = [_to_fm(h[c // 4, (c % 4) * NTOK:(c % 4 + 1) * NTOK]) for c in range(8)]
    zref = None
    for l in range(2):
        gfm = lambda v: np.ascontiguousarray(v.reshape(KC, 128).T)
        nc = bass.Bass("TRN2", target_bir_lowering=False)
        build_p1(nc)
        w1 = np.ascontiguousarray(inp["w_in"][l][:, :MIXC])
        g1 = gfm(inp["norm_mix"][l])
        r1 = _run(nc, [dict(hT=hT[c], gain=g1, w=w1) for c in range(8)])
        pmT = [np.concatenate([r1[b * 4 + q]["pm"].reshape(MIXC, NTOK) for q in range(4)], axis=1) for b in range(B)]
        ncm = build_mixer_prog(l)
        ims = []
        for c in range(8):
            b, g = c // 4, c % 4
            pc = np.ascontiguousarray(np.stack([pmT[b][512 * i + 128 * g:512 * i + 128 * (g + 1)] for i in range(14)]))
            pt = np.ascontiguousarray(np.stack([pmT[b][512 * i + 128 * g:512 * i + 128 * (g + 1)].T for i in (7, 8, 12, 13)]))
            d = dict(pc=pc, pt=pt); d.update(mixer_params(inp, l, g))
            ims.append(d)
        rm = _run(ncm, ims)
        zT = []
        for c in range(8):
            b, q = c // 4, c % 4
            ts_ = slice(q * NTOK, (q + 1) * NTOK)
            z = np.empty((128, KC, NTOK), np.float32)
            for g in range(4):
                r = rm[b * 4 + g]
                zb_ = r["zb"]
                if zref and l == 0:
                    zb_ = np.load(zref, mmap_mode="r")[b][:, g * 128:(g + 1) * 128].T
                z[:, 0 + g, :] = r["za"][:, ts_]
                z[:, 4 + g, :] = zb_[:, ts_]
                z[:, 8 + g, :] = r["zc"][ts_].T
                z[:, 12 + g, :] = r["zd"][ts_].T
            zT.append(z)
        last = (l == 1)
        nc3 = bass.Bass("TRN2", target_bir_lowering=False)
        build_p3(nc3, last)
        gains = np.ascontiguousarray(np.stack([gfm(inp["norm_mix"][l]), gfm(inp["norm_ffn"][l]), gfm(inp["norm_final"])], axis=1))
        com = dict(gains=gains, wg=np.ascontiguousarray(inp["w_in"][l][:, MIXC:]), wbo=inp["w_branch_out"][l], wout=inp["w_out"][l],
                   wfg=inp["ffn_w_gate"][l], wfu=inp["ffn_w_up"][l], wfd=inp["ffn_w_down"][l])
        r3 = _run(nc3, [dict(hT=hT[c], zT=zT[c], **com) for c in range(8)])
        hT = [r3[c]["hout"] for c in range(8)]
        if last:
            fin = [r3[c]["hfin"] for c in range(8)]
    out = np.empty((B, 8192, D), np.float32)
    for b in range(B):
        full = np.concatenate([fin[b * 4 + q].transpose(1, 0, 2).reshape(D, NTOK) for q in range(4)], axis=1)
        out[b] = full[:, 16:].T
    return out
```

```python
import contextlib
import numpy as np
import concourse.bass as bass
import concourse.mybir as mybir
from concourse.bass_utils import run_bass_kernel_spmd

F32 = mybir.dt.float32
BF16 = mybir.dt.bfloat16
I32 = mybir.dt.int32
AF = mybir.ActivationFunctionType
ALU = mybir.AluOpType
AX = mybir.AxisListType

ENGS = ("sync", "act", "dve", "pe", "pool")
SFX = [""]
CUR_FLUSH = [0]


class Prog:
    NDMASEM = 8

    def __init__(self, nc):
        self.nc = nc
        self.ops = []
        self.last_w = {}
        self.readers = {}
        self.last_eng = {}
        self.pending = {}

    def op(self, eng, fn, reads=(), writes=(), dma=False, cc=False):
        i = len(self.ops)
        deps = set()
        for k in reads:
            if k in self.last_w:
                deps.add(self.last_w[k])
        for k in writes:
            if k in self.last_w:
                deps.add(self.last_w[k])
            for r in self.readers.get(k, ()):
                deps.add(r)
        if eng in self.pending:
            deps |= self.pending.pop(eng)
        deps.discard(i)
        self.last_eng[eng] = i
        self.ops.append(dict(eng=eng, fn=fn, deps=deps, dma=dma, signal=False, cc=cc))
        for k in reads:
            self.readers.setdefault(k, []).append(i)
        for k in writes:
            self.last_w[k] = i
            self.readers[k] = []
        return i

    def barrier(self):
        if not hasattr(self, "marks"):
            self.marks = []
        self.marks.append(len(self.ops))

    def dma(self, eng, out, in_, reads=(), writes=(), **kw):
        return self.op(eng, lambda e: e.dma_start(out=out, in_=in_, **kw), reads, writes, dma=True)

    def setup(self, gst):
        nc = self.nc
        self.sems = {}
        for e in ENGS:
            self.sems[("eng", e)] = gst.enter_context(nc.semaphore("s_" + e))
        for e in ("sync", "act", "pool"):
            for k in range(self.NDMASEM):
                self.sems[("dma", e, k)] = gst.enter_context(nc.semaphore("d_%s_%d" % (e, k)))
        for k in range(self.NCC):
            self.sems[("cc", k)] = gst.enter_context(nc.semaphore("cc_%d" % k))
        self.cnt = {e: 0 for e in ENGS}
        self.dma_count = {e: 0 for e in ENGS}
        self.ncc = 0
        self.flushed = 0
        self.waited = {e: {} for e in ENGS}
        self.dma_hist = {e: [] for e in ENGS}

    NCC = 6

    def emit(self):
        import contextlib
        with contextlib.ExitStack() as gst:
            self.setup(gst)
            self.flush()

    EST = dict(pe=0.22, act=0.45, dve=0.45, pool=0.6, sync=0.1)

    def _schedule(self, lo, new):
        import heapq
        ops = self.ops
        marks = sorted(set(m for m in getattr(self, "marks", []) if lo < m < len(ops)))
        bounds = [lo] + marks + [len(ops)]
        order = []
        prev_tail = set()
        for si in range(len(bounds) - 1):
            seg = list(range(bounds[si], bounds[si + 1]))
            if not seg:
                continue
            segset = set(seg)
            if prev_tail:
                for i in seg:
                    ops[i]["deps"] |= prev_tail
            if not self.reorder:
                sched = seg
            else:
                ndep = {}; users = {i: [] for i in seg}
                for i in seg:
                    ds = [d for d in ops[i]["deps"] if d in segset]
                    ndep[i] = len(ds)
                    for d in ds:
                        users[d].append(i)
                fin = {}
                readyt = {i: 0.0 for i in seg}
                pend = {e: [] for e in ENGS}; rdy = {e: [] for e in ENGS}; tfree = {e: 0.0 for e in ENGS}
                for i in seg:
                    if ndep[i] == 0:
                        heapq.heappush(pend[ops[i]["eng"]], (0.0, i))
                sched = []
                left = len(seg)
                while left:
                    best = None
                    for e in ENGS:
                        while pend[e] and pend[e][0][0] <= tfree[e]:
                            t, i = heapq.heappop(pend[e]); heapq.heappush(rdy[e], i)
                        if rdy[e]:
                            cand = (tfree[e], rdy[e][0], e, True)
                        elif pend[e]:
                            cand = (pend[e][0][0], pend[e][0][1], e, False)
                        else:
                            continue
                        if best is None or cand[:2] < best[:2]:
                            best = cand
                    t0, i, e, isr = best
                    if isr:
                        heapq.heappop(rdy[e])
                    else:
                        heapq.heappop(pend[e])
                    o = ops[i]
                    if o["cc"]:
                        dur = 400.0
                    elif o["dma"]:
                        dur = 0.1; lat = 2.5
                    else:
                        dur = self.EST[e]
                    tfree[e] = t0 + dur
                    fin_t = t0 + dur + (2.5 if o["dma"] else 0.0)
                    fin[i] = fin_t
                    sched.append(i); left -= 1
                    for u in users[i]:
                        ue = ops[u]["eng"]
                        rt = fin_t + (0.0 if (ue == e and not o["dma"]) else 0.6)
                        if rt > readyt[u]:
                            readyt[u] = rt
                        ndep[u] -= 1
                        if ndep[u] == 0:
                            heapq.heappush(pend[ue], (readyt[u], u))
            order.extend(sched)
            last = {}
            tail = set()
            for i in sched:
                if ops[i]["dma"] or ops[i]["cc"]:
                    tail.add(i)
                else:
                    last[ops[i]["eng"]] = i
            prev_tail = tail | set(last.values())
        self.marks = []
        perm = {old: lo + k for k, old in enumerate(order)}
        newops = [ops[old] for old in order]
        for o in newops:
            o["deps"] = {perm.get(d, d) for d in o["deps"]}
        ops[lo:] = newops
        return list(range(lo, len(ops)))

    reorder = True

    def flush(self):
        nc = self.nc
        ops = self.ops
        lo = self.flushed
        new = list(range(lo, len(ops)))
        if not new:
            return
        CUR_FLUSH[0] += 1
        new = self._schedule(lo, new)
        for i in new:
            o = ops[i]
            if o["dma"]:
                e = o["eng"]
                j = self.dma_count[e]
                o["dma_idx"] = j
                self.dma_count[e] += 1
                hist = self.dma_hist[e]
                if j >= self.NDMASEM:
                    o["deps"].add(hist[j - self.NDMASEM])
                hist.append(i)
        for i in new:
            o = ops[i]
            best = {}
            keep = set()
            for d in o["deps"]:
                if d < lo:
                    continue
                po = ops[d]
                if po["dma"] or po["cc"]:
                    keep.add(d)
                    continue
                if po["eng"] == "pe" and o["eng"] == "pe" and not o["dma"]:
                    continue
                if d > best.get(po["eng"], -1):
                    best[po["eng"]] = d
            for e_, d in best.items():
                ops[d]["signal"] = True
                keep.add(d)
            o["deps"] = keep
        last_compute = {}
        for i in new:
            o = ops[i]
            if not o["dma"] and not o["cc"]:
                last_compute[o["eng"]] = i
        for e, i in last_compute.items():
            ops[i]["signal"] = True
        for i in new:
            o = ops[i]
            if o["cc"]:
                o["sem"] = ("cc", self.ncc); o["val"] = 1; self.ncc += 1
                assert self.ncc <= self.NCC
            elif o["dma"]:
                j = o["dma_idx"]
                o["sem"] = ("dma", o["eng"], j % self.NDMASEM)
                o["val"] = 16 * (j // self.NDMASEM + 1)
            elif o["signal"]:
                self.cnt[o["eng"]] += 1
                o["sem"] = ("eng", o["eng"])
                o["val"] = self.cnt[o["eng"]]
        final = {}
        for e in ENGS:
            if self.cnt[e]:
                final[("eng", e)] = self.cnt[e]
            for k in range(self.NDMASEM):
                n = len([1 for j in range(self.dma_count[e]) if j % self.NDMASEM == k])
                if n:
                    final[("dma", e, k)] = 16 * n
        sems = self.sems
        used = [e for e in ENGS if any(ops[i]["eng"] == e for i in new)]
        with nc.Block() as block:
            def make(e):
                def body(eng):
                    waited = self.waited[e]
                    for i in new:
                        o = ops[i]
                        if o["eng"] != e:
                            continue
                        need = {}
                        for d in o["deps"]:
                            if d < lo:
                                continue
                            po = ops[d]
                            if "sem" not in po:
                                continue
                            sk = po["sem"]
                            need[sk] = max(need.get(sk, 0), po["val"])
                        for sk, v in need.items():
                            if waited.get(sk, 0) >= v:
                                continue
                            eng.wait_ge(sems[sk], v)
                            waited[sk] = v
                        ins = o["fn"](eng)
                        if o["cc"]:
                            ins.then_inc(sems[o["sem"]])
                            eng.wait_ge(sems[o["sem"]], 1)
                            waited[o["sem"]] = 1
                        elif o["dma"]:
                            ins.then_inc(sems[o["sem"]], 16)
                        elif o["signal"]:
                            ins.then_inc(sems[o["sem"]], 1)
                    for sk, v in final.items():
                        if waited.get(sk, 0) < v:
                            eng.wait_ge(sems[sk], v)
                            waited[sk] = v
                return body

            reg = dict(sync=block.sync, act=block.scalar, dve=block.vector, pe=block.tensor, pool=block.gpsimd)
            for e in ENGS:
                reg[e](make(e))
        self.flushed = len(ops)
        self.last_w = {}
        self.readers = {}
        self.pending = {}
        self.last_eng = {}


def _keys(items):
    out = []
    for it in items:
        if isinstance(it, (str, tuple)):
            out.append(it)
        elif hasattr(it, "tensor"):
            out.append(it.tensor.name)
        else:
            out.append(it.name)
    return out


class PX(Prog):
    def _rw(self, r, w, dr, dw):
        return _keys(dr if r is None else r), _keys(dw if w is None else w)

    def ld(self, eng, out, in_, r=None, w=None, **kw):
        rr, ww = self._rw(r, w, [in_], [out])
        return self.op(eng, lambda e: e.dma_start(out=out, in_=in_, **kw), rr, ww, dma=True)

    def allgather(self, out, in_, groups, r=None, w=None):
        rr, ww = self._rw(r, w, [in_], [out])
        return self.op("pool", lambda e: e.collective_compute("AllGather", ALU.bypass, replica_groups=groups, ins=[in_], outs=[out]), rr, ww, cc=True)

    def mm(self, out, lhsT, rhs, start=True, stop=True, r=None, w=None, **kw):
        rr, ww = self._rw(r, w, [lhsT, rhs], [out])
        return self.op("pe", lambda e: e.matmul(out, lhsT=lhsT, rhs=rhs, start=start, stop=stop, **kw), rr, ww)

    def tr(self, out, in_, ident, r=None, w=None):
        rr, ww = self._rw(r, w, [in_, ident], [out])
        return self.op("pe", lambda e: e.transpose(out, in_, ident), rr, ww)

    def act(self, out, in_, func, bias=None, scale=None, accum_out=None, r=None, w=None, eng="act"):
        kw = {}
        rd = [in_]
        wd = [out]
        if bias is not None:
            kw["bias"] = bias
            if not isinstance(bias, (int, float)):
                rd.append(bias)
        if scale is not None:
            kw["scale"] = scale
            if not isinstance(scale, (int, float)):
                rd.append(scale)
        if accum_out is not None:
            kw["accum_out"] = accum_out
            wd.append(accum_out)
        rr, ww = self._rw(r, w, rd, wd)
        return self.op(eng, lambda e: e.activation(out=out, in_=in_, func=func, **kw), rr, ww)

    def tt(self, out, in0, in1, op, eng="dve", r=None, w=None):
        rr, ww = self._rw(r, w, [in0, in1], [out])
        return self.op(eng, lambda e: e.tensor_tensor(out=out, in0=in0, in1=in1, op=op), rr, ww)

    def ts(self, out, in0, s1, s2, op0, op1=None, eng="dve", r=None, w=None, accum_out=None):
        rd = [in0] + [s for s in (s1, s2) if s is not None and not isinstance(s, (int, float))]
        wd = [out] + ([accum_out] if accum_out is not None else [])
        rr, ww = self._rw(r, w, rd, wd)
        kw = {}
        if op1 is not None:
            kw["op1"] = op1
        if accum_out is not None:
            kw["accum_out"] = accum_out
        return self.op(eng, lambda e: e.tensor_scalar(out=out, in0=in0, scalar1=s1, scalar2=s2, op0=op0, **kw), rr, ww)

    def stt(self, out, in0, scalar, in1, op0, op1, eng="dve", r=None, w=None):
        rd = [in0, in1] + ([] if isinstance(scalar, (int, float)) else [scalar])
        rr, ww = self._rw(r, w, rd, [out])
        return self.op(eng, lambda e: e.scalar_tensor_tensor(out=out, in0=in0, scalar=scalar, in1=in1, op0=op0, op1=op1), rr, ww)

    def scan(self, out, d0, d1, init=0.0, op0=None, op1=None, r=None, w=None):
        rr, ww = self._rw(r, w, [d0, d1], [out])
        return self.op("dve", lambda e: e.tensor_tensor_scan(out=out, data0=d0, data1=d1, initial=init, op0=op0 or ALU.mult, op1=op1 or ALU.add), rr, ww)

    def cp(self, out, in_, eng="dve", r=None, w=None):
        rr, ww = self._rw(r, w, [in_], [out])
        if eng == "act":
            return self.op(eng, lambda e: e.copy(out=out, in_=in_), rr, ww)
        return self.op(eng, lambda e: e.tensor_copy(out=out, in_=in_), rr, ww)

    def recip(self, out, in_, r=None, w=None):
        rr, ww = self._rw(r, w, [in_], [out])
        return self.op("dve", lambda e: e.reciprocal(out=out, in_=in_), rr, ww)

    def memset(self, out, val, eng="dve", w=None):
        rr, ww = self._rw([], w, [], [out])
        return self.op(eng, lambda e: e.memset(out, val), rr, ww)


def rev_ap(ap):
    (ps, pn), (fs, fn) = ap.ap
    return bass.AP(ap.tensor, ap.offset + fs * (fn - 1), [[ps, pn], [-fs, fn]])


T = 8208
TP = 8320
PADL = 112
NT = 65
NBUF = 5
BW = 8320


def alloc_big(nc, st):
    return [st.enter_context(nc.sbuf_tensor("BIG%d" % i + SFX[0], [128, BW], F32)) for i in range(NBUF)]


def mixer_A(P, nc, st, big, psb, sm, pc, prm, zout, q="sync"):
    X, XC, A_, B_, HB = [b for b in big]
    sb = lambda name, shape, dt=F32: st.enter_context(nc.sbuf_tensor("s_" + name + SFX[0], shape, dt))
    cw = sb("a_cw", [128, 4]); cb = sb("a_cb", [128, 1])
    wa = sm["p512"][0][:, 0:256].rearrange("p (a b) -> p a b", a=2); wx = sm["p512"][1][:, 0:256].rearrange("p (a b) -> p a b", a=2)
    ba = sb("a_ba", [128, 2]); bx = sb("a_bx", [128, 2]); lam = sb("a_lam", [128, 2]); cch = sb("a_cch", [128, 2])
    gr, gi, sq = sm["p512"][2:5]
    ps_r = psb[0:2]
    ps_i = psb[2:4]
    P.ld(q, cw[:], prm["a_cw"]); P.ld(q, cb[:], prm["a_cb"])
    P.ld(q, wa, prm["a_wa"]); P.ld(q, wx, prm["a_wx"])
    P.ld(q, ba[:], prm["a_ba"]); P.ld(q, bx[:], prm["a_bx"]); P.ld(q, lam[:], prm["a_lam"])
    P.act(cch[:], lam[:], AF.Exp, scale=-1.0)
    P.act(cch[:], cch[:], AF.Ln, bias=1.0)
    P.ts(cch[:], cch[:], -8.0, None, ALU.mult)
    P.memset(X[:, 0:2], 0.0); P.memset(X[:, 2 + T:2 + T + 2], 0.0)
    P.ld(q, X[:, 2:2 + T], pc[0])
    P.act(XC[:, 0:T], X[:, 2:2 + T], AF.Identity, bias=cb[:, 0:1], scale=cw[:, 2:3])
    for k, off in ((0, 0), (1, 1), (3, 3)):
        P.stt(XC[:, 0:T], X[:, off:off + T], cw[:, k:k + 1], XC[:, 0:T], ALU.mult, ALU.add)
    ntile = (T + 511) // 512
    for d in range(2):
        for ti in range(ntile):
            t0 = ti * 512; n = min(512, T - t0); j = ti % 2
            P.mm(ps_r[j][:, 0:n], wa[:, d, :], XC[:, t0:t0 + n])
            P.mm(ps_i[j][:, 0:n], wx[:, d, :], XC[:, t0:t0 + n])
            P.act(gr[:, 0:n], ps_r[j][:, 0:n], AF.Sigmoid, bias=ba[:, d:d + 1])
            P.act(gi[:, 0:n], ps_i[j][:, 0:n], AF.Sigmoid, bias=bx[:, d:d + 1])
            P.act(A_[:, t0:t0 + n], gr[:, 0:n], AF.Exp, scale=cch[:, d:d + 1])
            P.act(sq[:, 0:n], A_[:, t0:t0 + n], AF.Square)
            P.act(sq[:, 0:n], sq[:, 0:n], AF.Sqrt, bias=1.0, scale=-1.0)
            P.tt(gi[:, 0:n], gi[:, 0:n], sq[:, 0:n], ALU.mult)
            P.tt(B_[:, t0:t0 + n], gi[:, 0:n], XC[:, t0:t0 + n], ALU.mult)
        if d == 0:
            P.scan(X[:, 0:T], A_[:, 0:T], B_[:, 0:T])
        else:
            P.scan(rev_ap(HB[:, 0:T]), rev_ap(A_[:, 0:T]), rev_ap(B_[:, 0:T]))
    P.tt(X[:, 0:T], X[:, 0:T], HB[:, 0:T], ALU.add)
    P.ld(q, A_[:, 0:T], pc[1])
    P.act(B_[:, 0:T], A_[:, 0:T], AF.Gelu_apprx_tanh)
    P.tt(X[:, 0:T], X[:, 0:T], B_[:, 0:T], ALU.mult)
    P.ld(q, zout, X[:, 0:T])


def prep_A(inp, l, g):
    sl = slice(g * 128, (g + 1) * 128)
    d = {}
    d["a_cw"] = np.ascontiguousarray(inp["lru_conv_w"][l][:, sl].T)
    d["a_cb"] = np.ascontiguousarray(inp["lru_conv_b"][l][sl][:, None])
    for nm, key in (("a_wa", "lru_wa"), ("a_wx", "lru_wx")):
        w = np.zeros((128, 2, 128), np.float32)
        for dd in range(2):
            for kb in range(2):
                w[kb * 64:(kb + 1) * 64, dd, kb * 64:(kb + 1) * 64] = inp[key][l][dd, 2 * g + kb]
        d[nm] = w
    d["a_ba"] = np.ascontiguousarray(inp["lru_ba"][l][:, sl].T)
    d["a_bx"] = np.ascontiguousarray(inp["lru_bx"][l][:, sl].T)
    d["a_lam"] = np.ascontiguousarray(inp["lru_lambda"][l][:, sl].T)
    return d


def alloc_small(nc, st):
    p128 = [st.enter_context(nc.sbuf_tensor("SM%d" % i + SFX[0], [128, 128], F32)) for i in range(36)]
    p512 = [st.enter_context(nc.sbuf_tensor("SL%d" % i + SFX[0], [128, 512], F32)) for i in range(5)]
    rmb = st.enter_context(nc.sbuf_tensor("RMB" + SFX[0], [128, TP], BF16))
    return dict(p128=p128, p512=p512, rmb=rmb)


class Pool128:
    def __init__(self, lst):
        self.lst = list(lst); self.i = 0

    def get(self, n=1):
        out = self.lst[self.i:self.i + n]; self.i += n
        assert self.i <= len(self.lst)
        return out if n > 1 else out[0]


def alloc_psum(nc, st):
    return [st.enter_context(nc.psum_tensor("PSB%d" % i + SFX[0], [128, 512], F32)) for i in range(8)]


def load_tok_padded(P, q, dst3, src):
    P.memset(dst3[0:PADL, 0, :], 0.0, w=[dst3])
    P.ld(q, dst3[PADL:128, 0, :], src[0:16, :], w=[dst3])
    P.ld(q, dst3[:, 1:NT, :], src[16:T, :].rearrange("(n p) d -> p n d", p=128), w=[dst3])


def mixer_C(P, nc, st, big, psb, sm, pc, pt, prm, zout, q="sync"):
    Q, KS, SBALL, CS, K = big
    pl = Pool128(sm["p128"])
    sb = lambda name, shape, dt=F32: st.enter_context(nc.sbuf_tensor("s_" + name + SFX[0], shape, dt))
    cm = sm["p512"][0][:, 0:384].rearrange("p (a b) -> p a b", a=3); cv = sb("c_v", [128, 4]); ident = pl.get()
    kbs = pl.get(2); pt_ = pl.get(2); qf = pl.get(2); qb = pl.get(2); sf = pl.get(2); osb = pl.get(2)
    junk = pl.get(); gt = pl.get(2); vt = pl.get(3)
    st4 = [sb("c_st%d" % i, [128, 4]) for i in range(2)]
    P.ld(q, cm, prm["c_m"]); P.ld(q, cv[:], prm["c_vec"]); P.ld(q, ident[:], prm["ident"])

    def ldv(i, n_):
        vj = n_ % 3
        if i == 0:
            P.memset(vt[vj][:], 0.0)
            P.ld("act", vt[vj][PADL:128, :], pt[0][0:16, :])
        else:
            P.ld("act", vt[vj][:], pt[0][16 + (i - 1) * 128:16 + i * 128, :])
        return vt[vj]
    SB3 = SBALL[:, 0:NT * 128].rearrange("p (n d) -> p n d", d=128)
    for dst, src in ((Q, pc[5]), (K, pc[6])):
        P.memset(dst[:, 0:PADL], 0.0); P.memset(KS[:, 0:PADL], 0.0)
        P.ld(q, dst[:, PADL:TP], src)
        P.ld(q, KS[0:64, PADL:TP], src[64:128, :], w=[KS]); P.ld("act", KS[64:128, PADL:TP], src[0:64, :], w=[KS])
        P.ld(q, CS[:, 0:TP], prm["c_cos"])
        P.tt(dst[:, 0:TP], dst[:, 0:TP], CS[:, 0:TP], ALU.mult)
        P.ld(q, CS[:, 0:TP], prm["c_sin"])
        P.tt(KS[:, 0:TP], KS[:, 0:TP], CS[:, 0:TP], ALU.mult)
        P.tt(dst[:, 0:TP], dst[:, 0:TP], KS[:, 0:TP], ALU.add)
    P.act(K[:, 0:TP], K[:, 0:TP], AF.Copy, scale=float(128 ** -0.5))
    P.memset(SB3[:, NT - 1, :], 0.0, w=[SBALL])
    nv = 0
    for i in range(NT - 1, 0, -1):
        j = i % 2
        vti = ldv(i, nv); nv += 1
        P.tr(psb[j][:, 0:128], K[:, i * 128:(i + 1) * 128], ident[:])
        P.act(kbs[j][:], psb[j][:, 0:128], AF.Copy, scale=cv[:, 1:2])
        P.mm(psb[2 + j][:, 0:128], kbs[j][:], vti[:])
        P.stt(SB3[:, i - 1, :], SB3[:, i, :], cv[:, 2:3], psb[2 + j][:, 0:128], ALU.mult, ALU.add,
              r=[("sball", i), psb[2 + j], cv], w=[("sball", i - 1)])
    P.memset(sf[0][:], 0.0)
    for i in range(NT):
        j = i % 2
        sl = slice(i * 128, (i + 1) * 128)
        vti = ldv(i, nv); nv += 1
        P.mm(psb[4 + j][:, 0:128], K[:, sl], Q[:, sl])
        P.tt(pt_[j][:], psb[4 + j][:, 0:128], cm[:, 0, :], ALU.mult)
        P.tt(qf[j][:], Q[:, sl], cm[:, 1, :], ALU.mult, eng="pool")
        P.tt(qb[j][:], Q[:, sl], cm[:, 2, :], ALU.mult, eng="pool")
        P.mm(psb[6 + j][:, 0:128], pt_[j][:], vti[:], start=True, stop=False)
        P.mm(psb[6 + j][:, 0:128], qf[j][:], sf[j][:], start=False, stop=False)
        P.mm(psb[6 + j][:, 0:128], qb[j][:], SB3[:, i, :], start=False, stop=True, r=[qb[j], ("sball", i), SBALL], w=[psb[6 + j]])
        if i < NT - 1:
            P.tr(psb[j][:, 0:128], K[:, sl], ident[:])
            P.act(kbs[j][:], psb[j][:, 0:128], AF.Copy, scale=cv[:, 0:1])
            P.mm(psb[2 + j][:, 0:128], kbs[j][:], vti[:])
            P.stt(sf[1 - j][:], sf[j][:], cv[:, 2:3], psb[2 + j][:, 0:128], ALU.mult, ALU.add)
        if i == 0:
            P.memset(gt[j][:], 0.0)
            P.ld("act", gt[j][PADL:128, :], pt[1][0:16, :])
        else:
            P.ld("act", gt[j][:], pt[1][16 + (i - 1) * 128:16 + i * 128, :])
        P.act(gt[j][:], gt[j][:], AF.Silu)
        P.memset(st4[j][:], 0.0)
        P.act(osb[j][:], psb[6 + j][:, 0:128], AF.Identity, accum_out=st4[j][:, 0:1])
        P.ts(st4[j][:, 1:2], st4[j][:, 0:1], -1.0 / 128, None, ALU.mult)
        P.act(junk[:], osb[j][:], AF.Square, bias=st4[j][:, 1:2], accum_out=st4[j][:, 2:3])
        P.act(st4[j][:, 3:4], st4[j][:, 2:3], AF.Sqrt, bias=1e-6, scale=1.0 / 128)
        P.recip(st4[j][:, 3:4], st4[j][:, 3:4])
        P.ts(osb[j][:], osb[j][:], st4[j][:, 1:2], st4[j][:, 3:4], ALU.add, ALU.mult)
        P.tt(osb[j][:], osb[j][:], gt[j][:], ALU.mult)
        if i == 0:
            P.ld(q, zout[0:16, :], osb[j][PADL:128, :])
        else:
            P.ld(q, zout[16 + (i - 1) * 128:16 + i * 128, :], osb[j][:])


def prep_C(g):
    d = {}
    gam = 1.0 - 2.0 ** (-5.0 - g)
    r = np.arange(128, dtype=np.float64)
    m = gam ** np.abs(r[:, None] - r[None, :])
    g1 = np.broadcast_to(gam ** (r + 1)[None, :], (128, 128))
    g2 = np.broadcast_to(gam ** (128 - r)[None, :], (128, 128))
    d["c_m"] = np.ascontiguousarray(np.stack([m, g1, g2], axis=1)).astype(np.float32)
    d["c_vec"] = np.stack([gam ** (127 - r), gam ** r, np.full(128, gam ** 128), np.zeros(128)], axis=1).astype(np.float32)
    d["ident"] = np.eye(128, dtype=np.float32)
    inv = (10000.0 ** (-np.arange(0, 128, 2, dtype=np.float32) / 128)).astype(np.float32)
    ang = (np.arange(T, dtype=np.float32)[:, None] * inv[None, :]).astype(np.float32)
    cos = np.cos(ang.astype(np.float64)).T; sin = np.sin(ang.astype(np.float64)).T
    c2 = np.zeros((128, TP), np.float32); s2 = np.zeros((128, TP), np.float32)
    c2[0:64, PADL:] = cos; c2[64:, PADL:] = cos
    s2[0:64, PADL:] = -sin; s2[64:, PADL:] = sin
    d["c_cos"] = c2; d["c_sin"] = s2
    return d


def bcast_last(ap2, idx, nblk=4, blk=32):
    (ps, pn), (fs, fn) = ap2.ap
    return bass.AP(ap2.tensor, ap2.offset + idx * fs, [[ps, pn], [blk * fs, nblk], [0, blk]])


def blk_view(ap2, blk=32):
    return ap2.rearrange("p (c r) -> p c r", r=blk)


def mixer_D(P, nc, st, big, psb, sm, pc, pt, prm, zout, layer, q="sync"):
    Q, KK, TMP, BC, OACC = big
    RM = sm["rmb"]
    pl = Pool128(sm["p128"])
    sb = lambda name, shape, dt=F32: st.enter_context(nc.sbuf_tensor("s_" + name + SFX[0], shape, dt))
    ident = pl.get(); msk = sm["p512"][0][:, 0:256].rearrange("p (a b) -> p a b", a=2); cmask = sb("d_cm", [128, 4])
    lbl = sb("d_lbl", [128, 2]); lb = sb("d_lb", [128, 2])
    NR = 2
    d1 = pl.get(NR)
    e1 = pl.get(NR)
    e2 = pl.get(NR)
    e3 = pl.get(NR)
    e4 = pl.get(NR)
    qm = pl.get(NR)
    km = pl.get(NR)
    kh = pl.get(NR)
    QM = [sm["p512"][1 + i][:].rearrange("p (a b) -> p a b", a=4) for i in range(NR)]
    khT = [sm["p512"][3 + i][:].rearrange("p (a b) -> p a b", a=4) for i in range(NR)]
    ptm = pl.get(NR); vt = pl.get(3); gt = pl.get(2); osb = pl.get(2); junk = pl.get()
    st4 = [sb("d_st%d" % i, [128, 2]) for i in range(2)]
    NS = 8
    S = pl.get(NS)
    P.ld(q, ident[:], prm["ident"]); P.ld(q, msk, prm["d_msk"]); P.ld(q, cmask[:], prm["d_cm"]); P.ld(q, lbl[:], prm["d_lbl"])
    if layer == 0:
        P.memset(lb[:, 0:1], 0.0); P.memset(lb[:, 1:2], 1.0)
    else:
        P.tt(lb[:, 0:1], lbl[:, 1:2], lbl[:, 0:1], ALU.subtract)
        P.act(lb[:, 0:1], lb[:, 0:1], AF.Sigmoid)
        P.ts(lb[:, 1:2], lb[:, 0:1], -1.0, 1.0, ALU.mult, ALU.add)
    for i in range(NR):
        P.memset(QM[i], 0.0)
    OA3 = OACC[:, 0:NT * 128].rearrange("p (n d) -> p n d", d=128)
    P.memset(Q[:, 0:PADL], 0.0)
    P.ld(q, Q[:, PADL:TP], pc[9])
    P.act(Q[:, PADL:TP], Q[:, PADL:TP], AF.Silu)
    for ph in range(2):
        bwd = (ph == 0)
        P.memset(KK[:, 0:PADL], 0.0); P.memset(TMP[:, 0:PADL], 0.0)
        P.ld(q, TMP[:, PADL:TP], pc[11 if bwd else 10])
        P.act(TMP[:, PADL:TP], TMP[:, PADL:TP], AF.Sigmoid)
        P.ts(TMP[:, PADL:TP], TMP[:, PADL:TP], lb[:, 1:2], lb[:, 0:1], ALU.mult, ALU.add)
        P.ts(KK[:, PADL:TP], TMP[:, PADL:TP], -1.0, 1.0, ALU.mult, ALU.add)
        P.act(TMP[:, PADL:TP], TMP[:, PADL:TP], AF.Ln)
        P.memset(RM[:, 0:TP], 1.0)
        rm3 = RM[:, 0:TP].rearrange("p (c r) -> p c r", r=32)
        if bwd:
            P.memset(rm3[:, :, 31:32], 0.0)
            P.scan(rev_ap(BC[:, 0:TP]), rev_ap(RM[:, 0:TP]), rev_ap(TMP[:, 0:TP]))
        else:
            P.memset(rm3[:, :, 0:1], 0.0)
            P.scan(BC[:, 0:TP], RM[:, 0:TP], TMP[:, 0:TP])
        ilast = 0 if bwd else 31
        cur = 0
        P.memset(S[0][:], 0.0)
        tiles = range(NT - 1, -1, -1) if bwd else range(NT)
        tiles = list(tiles)
        for n_, t in enumerate(tiles):
            j = n_ % NR
            sl = slice(t * 128, (t + 1) * 128)
            bct = BC[:, sl]
            vj = n_ % 3
            if t == 0:
                P.memset(vt[vj][:], 0.0)
                P.ld("act", vt[vj][PADL:128, :], pt[2][0:16, :])
            else:
                P.ld("act", vt[vj][:], pt[2][16 + (t - 1) * 128:16 + t * 128, :])
            P.tt(blk_view(d1[j][:]), blk_view(bct), bcast_last(bct, 15), ALU.subtract)
            P.act(e1[j][:], d1[j][:], AF.Exp)
            P.act(e2[j][:], d1[j][:], AF.Exp, scale=-1.0)
            P.tt(qm[j][:], Q[:, sl], e1[j][:], ALU.mult, eng="pool")
            P.tt(km[j][:], KK[:, sl], e2[j][:], ALU.mult, eng="pool")
            P.act(e3[j][:], bct, AF.Exp)
            P.tt(blk_view(e4[j][:]), bcast_last(bct, ilast), blk_view(bct), ALU.subtract)
            P.act(e4[j][:], e4[j][:], AF.Exp)
            P.tt(kh[j][:], KK[:, sl], e4[j][:], ALU.mult)
            qmd = bass.AP(QM[j].tensor, QM[j].offset, [[QM[j].ap[0][0], 128], [160, 4], [1, 32]])
            P.tt(qmd, blk_view(Q[:, sl]), blk_view(e3[j][:]), ALU.mult, w=[QM[j]])
            P.mm(psb[j][:, 0:128], km[j][:], qm[j][:])
            P.tt(ptm[j][:], psb[j][:, 0:128], msk[:, 1 if bwd else 0, :], ALU.mult)
            P.tr(psb[2 + j][:, 0:128], kh[j][:], ident[:])
            for c in range(4):
                P.act(khT[j][:, c, :], psb[2 + j][:, 0:128], AF.Copy, scale=cmask[:, c:c + 1])
            for c in range(4):
                P.mm(psb[4 + j][:, c * 128:(c + 1) * 128], khT[j][:, c, :], vt[vj][:])
            P.mm(psb[6 + j][:, 0:128], ptm[j][:], vt[vj][:], start=True, stop=False)
            corder = range(3, -1, -1) if bwd else range(4)
            for ci, c in enumerate(corder):
                P.mm(psb[6 + j][:, 0:128], QM[j][:, c, :], S[cur][:], start=False, stop=(ci == 3))
                nxt = (cur + 1) % NS
                P.stt(S[nxt][:], S[cur][:], e3[j][:, c * 32 + ilast:c * 32 + ilast + 1], psb[4 + j][:, c * 128:(c + 1) * 128], ALU.mult, ALU.add)
                cur = nxt
            if bwd:
                P.cp(OA3[:, t, :], psb[6 + j][:, 0:128], eng="act", w=[("oacc", t)])
            else:
                oj = n_ % 2
                if t == 0:
                    P.memset(gt[oj][:], 0.0)
                    P.ld("act", gt[oj][PADL:128, :], pt[3][0:16, :])
                else:
                    P.ld("act", gt[oj][:], pt[3][16 + (t - 1) * 128:16 + t * 128, :])
                P.act(gt[oj][:], gt[oj][:], AF.Silu)
                P.tt(osb[oj][:], psb[6 + j][:, 0:128], OA3[:, t, :], ALU.add, r=[psb[6 + j], ("oacc", t)], w=[osb[oj]])
                P.memset(st4[oj][:], 0.0)
                P.act(junk[:], osb[oj][:], AF.Square, accum_out=st4[oj][:, 0:1])
                P.act(st4[oj][:, 1:2], st4[oj][:, 0:1], AF.Sqrt, bias=1e-6, scale=1.0 / 128)
                P.recip(st4[oj][:, 1:2], st4[oj][:, 1:2])
                P.stt(osb[oj][:], osb[oj][:], st4[oj][:, 1:2], gt[oj][:], ALU.mult, ALU.mult)
                if t == 0:
                    P.ld(q, zout[0:16, :], osb[oj][PADL:128, :])
                else:
                    P.ld(q, zout[16 + (t - 1) * 128:16 + t * 128, :], osb[oj][:])


def prep_D(inp, l, g):
    d = {}
    sl = slice(g * 128, (g + 1) * 128)
    d["ident"] = np.eye(128, dtype=np.float32)
    r = np.arange(128)
    same = (r[:, None] // 32) == (r[None, :] // 32)
    mf = same & (r[:, None] <= r[None, :])
    mb = same & (r[:, None] >= r[None, :])
    d["d_msk"] = np.ascontiguousarray(np.stack([mf, mb], axis=1)).astype(np.float32)
    d["d_cm"] = ((r[:, None] // 32) == np.arange(4)[None, :]).astype(np.float32)
    d["d_lbl"] = np.ascontiguousarray(inp["hgrn_lb_logits"][:, sl].T)
    return d


NFFT = 16384
NJ = 16416
TWO_PI = float(2 * np.pi)


def mixer_B(P, nc, st, big, psb, sm, pc, prm, zout, q="sync"):
    B0, B1, B2, X0C, B4 = big
    sb = lambda name, shape, dt=F32: st.enter_context(nc.sbuf_tensor("s_" + name + SFX[0], shape, dt))
    pl = Pool128(sm["p128"])
    UD = nc.dram_tensor("b_UD" + SFX[0], [128, NFFT], F32).ap()
    KD = nc.dram_tensor("b_KD" + SFX[0], [128, NJ], F32).ap()
    YD = nc.dram_tensor("b_YD" + SFX[0], [128, NFFT], F32).ap()
    cw = sb("b_cw", [128, 9]); cb = sb("b_cb", [128, 3]); bias = sb("b_bias", [128, 1]); nad = sb("b_nad", [128, 2])
    b12 = sb("b_b12", [64, 2]); negpi = sb("b_negpi", [128, 1])
    hbx = sb("b_hbx", [128, 32])
    w3 = sm["p512"][0][0:64, 0:256].rearrange("p (a b) -> p a b", a=2)
    F1 = sm["p512"][1][:, 0:256]
    IF1 = sm["p512"][2][:].rearrange("p (a b) -> p a b", a=2)
    TW = sm["p512"][3][:, 0:256].rearrange("p (a b) -> p a b", a=2)
    ICN = sm["p512"][4][:, 0:256].rearrange("p (a b) -> p a b", a=2)
    FS = pl.get(); ones1 = pl.get()[0:1, :]
    ztile = pl.get(2); tpos = pl.get(2); h1 = pl.get(2); h2 = pl.get(2); ets = pl.get(2)
    w1 = pl.get()[0:33, 0:64]; w2 = pl.get()[0:64, 0:64]
    rm32 = sm["rmb"][:].bitcast(F32)
    tmpL = [rm32[:, i * 512:(i + 1) * 512] for i in range(8)]
    P.ld(q, cw[:], prm["b_cw"]); P.ld(q, cb[:], prm["b_cb"]); P.ld(q, bias[:], prm["b_bias"]); P.ld(q, nad[:], prm["b_decay"])
    P.ld(q, w1, prm["b_w1"]); P.ld(q, w2, prm["b_w2"]); P.ld(q, b12[:], prm["b_b12"])
    P.ld(q, F1, prm["b_F1"]); P.ld(q, IF1, prm["b_IF1"]); P.ld(q, TW, prm["b_TW"]); P.ld(q, ICN, prm["b_ICN"]); P.ld(q, FS[:], prm["b_FS"])
    w3t = w3; P.ld(q, w3t, prm["b_w3"])
    P.memset(negpi[:], -float(np.pi)); P.memset(ones1, 1.0)
    P.act(nad[:], nad[:], AF.Abs)
    P.ts(nad[:], nad[:], -1.0, None, ALU.mult)
    bhq = sb("b_bhq", [64, 4])
    P.ts(bhq[:, 0:2], b12[:], 0.5, None, ALU.mult)
    P.ts(bhq[:, 2:4], b12[:], 0.25, None, ALU.mult)
    s4 = pl.get(2)
    outs = [X0C, B4, B0]
    for i in range(3):
        src = B1 if i < 2 else B2
        P.memset(src[:, 0:1], 0.0); P.memset(src[:, T + 1:T + 2], 0.0)
        P.ld(q, src[:, 1:1 + T], pc[2 + i])
        o = outs[i]
        P.act(o[:, 0:T], src[:, 1:1 + T], AF.Identity, bias=cb[:, i:i + 1], scale=cw[:, 3 * i + 1:3 * i + 2])
        P.stt(o[:, 0:T], src[:, 0:T], cw[:, 3 * i:3 * i + 1], o[:, 0:T], ALU.mult, ALU.add)
        P.stt(o[:, 0:T], src[:, 2:2 + T], cw[:, 3 * i + 2:3 * i + 3], o[:, 0:T], ALU.mult, ALU.add)
    P.tt(B4[:, 0:T], B4[:, 0:T], B0[:, 0:T], ALU.mult)
    P.memset(B1[:, 0:NFFT - T], 0.0)
    P.ld(q, UD[:, 0:T], B4[:, 0:T]); P.ld(q, UD[:, T:NFFT], B1[:, 0:NFFT - T])
    segs = []
    for lo, hi, d in ((0, T, 0), (T, NJ, 1)):
        j0 = lo
        while j0 < hi:
            n = min(128, hi - j0); segs.append((j0, n, d)); j0 += n
    for si, (j0, n, d) in enumerate(segs):
        j = si % 2
        P.ld(q, ztile[j][0:33, 0:n], prm["b_z"][:, j0:j0 + n]); P.ld("act", tpos[j][0:1, 0:n], prm["b_tpos"][:, j0:j0 + n])
        P.mm(psb[j][0:64, 0:n], w1, ztile[j][0:33, 0:n])
        P.act(h1[j][0:64, 0:n], psb[j][0:64, 0:n], AF.Sin, bias=bhq[:, 0:1], scale=0.5)
        P.act(s4[j][0:64, 0:n], psb[j][0:64, 0:n], AF.Sin, bias=bhq[:, 2:3], scale=0.25)
        P.tt(s4[j][0:64, 0:n], s4[j][0:64, 0:n], s4[j][0:64, 0:n], ALU.mult)
        P.ts(s4[j][0:64, 0:n], s4[j][0:64, 0:n], -2.0, 1.0, ALU.mult, ALU.add)
        P.stt(h1[j][0:64, 0:n], h1[j][0:64, 0:n], 2.0, s4[j][0:64, 0:n], ALU.mult, ALU.mult)
        P.mm(psb[2 + j][0:64, 0:n], w2, h1[j][0:64, 0:n])
        P.act(h2[j][0:64, 0:n], psb[2 + j][0:64, 0:n], AF.Sin, bias=bhq[:, 1:2], scale=0.5)
        P.act(s4[j][0:64, 0:n], psb[2 + j][0:64, 0:n], AF.Sin, bias=bhq[:, 3:4], scale=0.25)
        P.tt(s4[j][0:64, 0:n], s4[j][0:64, 0:n], s4[j][0:64, 0:n], ALU.mult)
        P.ts(s4[j][0:64, 0:n], s4[j][0:64, 0:n], -2.0, 1.0, ALU.mult, ALU.add)
        P.stt(h2[j][0:64, 0:n], h2[j][0:64, 0:n], 2.0, s4[j][0:64, 0:n], ALU.mult, ALU.mult)
        P.mm(psb[4 + j][:, 0:n], w3t[:, d, :], h2[j][0:64, 0:n])
        P.mm(psb[6 + j][:, 0:n], ones1, tpos[j][0:1, 0:n])
        et = ets[j]
        P.act(et[:, 0:n], psb[6 + j][:, 0:n], AF.Exp, scale=nad[:, d:d + 1])
        P.tt(et[:, 0:n], et[:, 0:n], psb[4 + j][:, 0:n], ALU.mult)
        P.ld(q, KD[:, j0:j0 + n], et[:, 0:n])
    P.ld(q, hbx[:, 0:32], KD[:, NFFT:NJ])

    def twiddle(ps, dre, dim_, inverse, kre, kim, jj):
        p3 = ps[:, 0:512].rearrange("p (c x) -> p c x", c=2) if hasattr(ps, "name") else ps.rearrange("p (c x) -> p c x", c=2)
        pre, pim = p3[:, :, 0:128], p3[:, :, 128:256]
        (ps_, pn_), _, (fs_, fn_) = TW.ap
        twc = bass.AP(TW.tensor, TW.offset, [[ps_, pn_], [0, 2], [fs_, 128]])
        tws = bass.AP(TW.tensor, TW.offset + 128 * fs_, [[ps_, pn_], [0, 2], [fs_, 128]])
        t = [sm_t.rearrange("p (c x) -> p c x", c=2) for sm_t in (tmpL[jj][:, 0:256], tmpL[jj][:, 256:512], tmpL[2 + jj][:, 0:256], tmpL[2 + jj][:, 256:512])]
        ps = psb[jj]
        P.tt(t[0], pre, twc, ALU.mult, r=[ps, TW], w=[("tw", jj, 0)])
        P.tt(t[1], pim, tws, ALU.mult, r=[ps, TW], w=[("tw", jj, 1)])
        P.tt(t[2], pre, tws, ALU.mult, r=[ps, TW], w=[("tw", jj, 2)])
        P.tt(t[3], pim, twc, ALU.mult, r=[ps, TW], w=[("tw", jj, 3)])
        if not inverse:
            P.tt(dre, t[0], t[1], ALU.subtract, eng="pool", r=[("tw", jj, 0), ("tw", jj, 1)], w=[kre])
            P.tt(dim_, t[2], t[3], ALU.add, eng="pool", r=[("tw", jj, 2), ("tw", jj, 3)], w=[kim])
        else:
            P.tt(dre, t[0], t[1], ALU.add, eng="pool", r=[("tw", jj, 0), ("tw", jj, 1)], w=[kre])
            P.tt(dim_, t[3], t[2], ALU.subtract, eng="pool", r=[("tw", jj, 2), ("tw", jj, 3)], w=[kim])

    Cm = F1[:, 0:128]; mS = F1[:, 128:256]

    def fwd(INv, GRE, GIM, nch, kin, kgre, kgim, post):
        for c2 in range(nch // 2):
            jj = c2 % 2
            for cc in range(2):
                ch = 2 * c2 + cc
                P.mm(psb[jj][:, cc * 256:(cc + 1) * 256], INv[:, ch, :], F1, r=[kin, F1], w=[psb[jj]])
            twiddle(psb[jj][:, 0:512], GRE[:, 2 * c2:2 * c2 + 2, :], GIM[:, 2 * c2:2 * c2 + 2, :], False, kgre, kgim, jj)
        GREf = GRE.rearrange("p c x -> p (c x)"); GIMf = GIM.rearrange("p c x -> p (c x)")
        for ti in range(nch * 128 // 512):
            jj = ti % 2
            sl = slice(ti * 512, (ti + 1) * 512)
            P.mm(psb[4 + jj][:, :], Cm, GREf[:, sl], start=True, stop=False, r=[F1, kgre], w=[psb[4 + jj]])
            P.mm(psb[4 + jj][:, :], FS[:], GIMf[:, sl], start=False, stop=True, r=[FS, kgim], w=[psb[4 + jj]])
            P.mm(psb[6 + jj][:, :], Cm, GIMf[:, sl], start=True, stop=False, r=[F1, kgim], w=[psb[6 + jj]])
            P.mm(psb[6 + jj][:, :], mS, GREf[:, sl], start=False, stop=True, r=[F1, kgre], w=[psb[6 + jj]])
            post(ti, psb[4 + jj], psb[6 + jj], jj)

    for hc in range(2):
        ch0 = 64 * hc
        P.barrier()
        IN = B0[:, 0:8192].rearrange("p (c x) -> p c x", x=128)
        P.ld(q, IN, KD[ch0:ch0 + 64, 0:NFFT].rearrange("c (a b) -> a c b", b=128), w=["kin"])
        GRE = B1[:, 0:8192].rearrange("p (c x) -> p c x", x=128); GIM = B2[:, 0:8192].rearrange("p (c x) -> p c x", x=128)

        def post_k(ti, pr, pi, jj):
            sl = slice(ti * 512, (ti + 1) * 512)
            P.cp(B0[:, sl], pr[:, :], eng="act", r=[pr, "kin"], w=[("kre", ti)])
            P.cp(B4[:, sl], pi[:, :], eng="dve", r=[pi], w=[("kim", ti)])
        fwd(IN, GRE, GIM, 64, "kin", "kgre", "kgim", post_k)
        for qq in range(2):
            c0 = ch0 + 32 * qq
            P.barrier()
            UIN = B1[:, 0:4096].rearrange("p (c x) -> p c x", x=128)
            UGRE = B1[:, 4096:8192].rearrange("p (c x) -> p c x", x=128)
            UGIM = B2[:, 0:4096].rearrange("p (c x) -> p c x", x=128)
            YIM = B2[:, 4096:8192]
            P.ld(q, UIN, UD[c0:c0 + 32, :].rearrange("c (a b) -> a c b", b=128), r=[UD, "kgre", "kgim"], w=["uin"])

            def post_u(ti, pr, pi, jj, qq=qq):
                sl = slice(ti * 512, (ti + 1) * 512)
                ks = slice(qq * 4096 + ti * 512, qq * 4096 + (ti + 1) * 512)
                kt = (qq * 4096) // 512 + ti
                t = [tmpL[4 + jj], tmpL[6 + jj]]
                P.tt(t[0], pr[:, :], B0[:, ks], ALU.mult, r=[pr, ("kre", kt)], w=[("pu", jj, 0)])
                P.tt(t[1], pi[:, :], B4[:, ks], ALU.mult, r=[pi, ("kim", kt)], w=[("pu", jj, 1)])
                P.tt(B1[:, sl], t[0], t[1], ALU.subtract, eng="pool", r=[("pu", jj, 0), ("pu", jj, 1), "uin"], w=[("yre", ti)])
                P.tt(t[0], pr[:, :], B4[:, ks], ALU.mult, r=[pr, ("kim", kt), ("pu", jj, 0)], w=[("pu", jj, 0)])
                P.tt(t[1], pi[:, :], B0[:, ks], ALU.mult, r=[pi, ("kre", kt), ("pu", jj, 1)], w=[("pu", jj, 1)])
                P.tt(YIM[:, sl], t[0], t[1], ALU.add, eng="pool", r=[("pu", jj, 0), ("pu", jj, 1)], w=[("yim", ti)])
            fwd(UIN, UGRE, UGIM, 32, "uin", "ugre", "ugim", post_u)
            YRE3 = B1[:, 0:4096].rearrange("p (c x) -> p c x", x=128); YIM3 = YIM.rearrange("p (c x) -> p c x", x=128)
            yre_keys = [("yre", ti) for ti in range(8)]; yim_keys = [("yim", ti) for ti in range(8)]
            for c2 in range(16):
                jj = c2 % 2
                for cc in range(2):
                    ch = 2 * c2 + cc
                    P.mm(psb[jj][:, cc * 256:(cc + 1) * 256], YRE3[:, ch, :], IF1[:, 0, :], start=True, stop=False, r=yre_keys + [IF1], w=[psb[jj]])
                    P.mm(psb[jj][:, cc * 256:(cc + 1) * 256], YIM3[:, ch, :], IF1[:, 1, :], start=False, stop=True, r=yim_keys + [IF1], w=[psb[jj]])
                twiddle(psb[jj][:, 0:512], UGRE[:, 2 * c2:2 * c2 + 2, :], UGIM[:, 2 * c2:2 * c2 + 2, :], True, "ugre", "ugim", jj)
            HRE = B1[:, 4096:8192]; HIM = B2[:, 0:4096]
            for ti in range(8):
                jj = ti % 2
                sl = slice(ti * 512, (ti + 1) * 512)
                P.mm(psb[4 + jj][0:65, :], ICN[:, 0, 0:65], HRE[:, sl], start=True, stop=False, r=[ICN, "ugre"], w=[psb[4 + jj]])
                P.mm(psb[4 + jj][0:65, :], ICN[:, 1, 0:65], HIM[:, sl], start=False, stop=True, r=[ICN, "ugim"], w=[psb[4 + jj]])
                P.cp(tmpL[4 + jj][0:65, :], psb[4 + jj][0:65, :], eng="act", r=[psb[4 + jj]], w=[("pu", jj, 0)])
                P.ld(q, YD[c0 + 4 * ti:c0 + 4 * ti + 4, 0:65 * 128].rearrange("c (a b) -> a c b", b=128),
                     tmpL[4 + jj][0:65, :].rearrange("p (c x) -> p c x", x=128), r=[("pu", jj, 0)], w=[YD])
    P.barrier()
    P.ld(q, B0[:, 0:T], YD[:, 0:T], r=[YD], w=[B0])
    P.ld(q, B1[:, 0:T], UD[:, 0:T], r=[UD], w=[B1])
    for m in range(31):
        P.stt(B0[:, 0:31 - m], B1[:, 8177 + m:T], hbx[:, m:m + 1], B0[:, 0:31 - m], ALU.mult, ALU.add)
    P.stt(B0[:, 0:T], B1[:, 0:T], bias[:, 0:1], B0[:, 0:T], ALU.mult, ALU.add)
    P.tt(B0[:, 0:T], B0[:, 0:T], X0C[:, 0:T], ALU.mult)
    P.ld(q, zout, B0[:, 0:T])


def prep_B(inp, l, g):
    d = {}
    sl = slice(g * 128, (g + 1) * 128)
    cw = np.zeros((128, 9), np.float32); cb = np.zeros((128, 3), np.float32)
    for i in range(3):
        cols = slice(512 * i + g * 128, 512 * i + (g + 1) * 128)
        cw[:, 3 * i:3 * i + 3] = inp["hy_conv_w"][l][:, cols].T
        cb[:, i] = inp["hy_conv_b"][l][cols]
    d["b_cw"] = cw; d["b_cb"] = cb
    d["b_bias"] = np.ascontiguousarray(inp["hy_bias"][l][sl][:, None])
    d["b_decay"] = np.ascontiguousarray(np.stack([inp["hy_decay"][l][0:512][sl], inp["hy_decay"][l][512:1024][sl]], axis=1))
    d["b_w1"] = np.ascontiguousarray(inp["hy_w1"][l]); d["b_w2"] = np.ascontiguousarray(inp["hy_w2"][l])
    d["b_b12"] = np.ascontiguousarray(np.stack([inp["hy_b1"][l], inp["hy_b2"][l]], axis=1))
    d["b_w3"] = np.ascontiguousarray(np.stack([inp["hy_w3"][l][:, 0:512][:, sl], inp["hy_w3"][l][:, 512:1024][:, sl]], axis=1))
    pos = np.zeros(NJ, np.int64)
    pos[0:T] = np.arange(T); pos[T:NFFT] = NFFT - np.arange(T, NFFT); pos[NFFT:NFFT + 31] = 8177 + np.arange(31)
    n = pos.astype(np.float32)
    t = (n / np.float32(T - 1)).astype(np.float32)
    freqs = np.linspace(1e-4, 15, 16, dtype=np.float32)
    ang = (np.float32(2.0 * np.pi) * n[:, None] * freqs[None, :] / np.float32(T)).astype(np.float32)
    z = np.concatenate([t[:, None], np.cos(ang), -np.sin(ang)], axis=1).astype(np.float32)
    d["b_z"] = np.ascontiguousarray(z.T); d["b_tpos"] = np.ascontiguousarray(t[None, :])
    a = np.arange(128, dtype=np.float64)
    th = 2 * np.pi * np.outer(a, a) / 128
    C, S = np.cos(th), np.sin(th)
    d["b_F1"] = np.concatenate([C, -S], axis=1).astype(np.float32)
    d["b_FS"] = S.astype(np.float32)
    d["b_IF1"] = np.ascontiguousarray(np.stack([np.concatenate([C, S], axis=1), np.concatenate([-S, C], axis=1)], axis=1)).astype(np.float32)
    tw = 2 * np.pi * np.outer(a, a) / NFFT
    d["b_TW"] = np.ascontiguousarray(np.stack([np.cos(tw), -np.sin(tw)], axis=1)).astype(np.float32)
    d["b_ICN"] = np.ascontiguousarray(np.stack([C / NFFT, -S / NFFT], axis=1)).astype(np.float32)
    return d


D = 2048
KC = 16
NTOK = 2052
HALF = 1026
NW = 342
DFF = 5632
JC = 44
MIXC = 7168
NWB = 8


def _rmsnorm(P, nc, env, h_dram, gain_sb, nT, half, fin=None):
    hs, sq, ones, PSS, rstd = env["hs"], env["sq"], env["ones"], env["PSA"], env["rstd"]
    c0 = half * HALF
    for k in range(KC):
        j = k % 2
        P.ld("sync", hs[j][:], h_dram[:, k, c0:c0 + HALF])
        P.act(sq[j][:], hs[j][:], AF.Square)
        for n in range(3):
            P.mm(PSS[:, n, 0:NW], ones[:], sq[j][:, n * NW:(n + 1) * NW], start=(k == 0), stop=(k == KC - 1))
    P.act(rstd[:].rearrange("p (n f) -> p n f", n=3), PSS[:, :, 0:NW], AF.Sqrt, bias=env["eps"][:, 0:1], scale=1.0 / D)
    P.recip(rstd[:], rstd[:])
    for k in range(KC):
        j = k % 2
        P.ld("sync", hs[j][:], h_dram[:, k, c0:c0 + HALF])
        if fin is None:
            P.stt(nT[:, k, :], hs[j][:], gain_sb[:, k:k + 1], rstd[:], ALU.mult, ALU.mult, w=[("nT", k)])
        else:
            P.stt(env["ot"][j][:], hs[j][:], gain_sb[:, k:k + 1], rstd[:], ALU.mult, ALU.mult)
            P.ld("sync", fin[:, k, c0:c0 + HALF], env["ot"][j][:])


def _wload(P, wt, w2d, nk, eng="pool"):
    P.ld(eng, wt[:, 0:nk, :], w2d.rearrange("(k p) m -> p k m", p=128))


def _common(nc, st):
    sb = lambda name, shape, dt=F32: st.enter_context(nc.sbuf_tensor(name + SFX[0], shape, dt))
    env = {}
    env["hs"] = [sb("hs%d" % i, [128, HALF]) for i in range(2)]
    env["sq"] = [sb("sq%d" % i, [128, HALF]) for i in range(2)]
    env["ones"] = sb("ones", [128, 128])
    env["rstd"] = sb("rstd", [128, HALF])
    env["eps"] = sb("eps", [128, 1])
    env["PSA"] = st.enter_context(nc.psum_tensor("PSA" + SFX[0], [128, 3, 512], F32))
    env["PSB"] = st.enter_context(nc.psum_tensor("PSB" + SFX[0], [128, 3, 512], F32))
    env["nT"] = sb("nT", [128, KC, HALF], BF16)
    env["wt"] = [sb("wt%d" % i, [128, KC, 128], BF16) for i in range(NWB)]
    env["ot"] = [sb("ot%d" % i, [128, HALF]) for i in range(2)]
    return env, sb


def build_p1(nc):
    hT = nc.dram_tensor("hT", [128, KC, NTOK], F32, kind="ExternalInput").ap()
    gain = nc.dram_tensor("gain", [128, KC], F32, kind="ExternalInput").ap()
    w = nc.dram_tensor("w", [D, MIXC], F32, kind="ExternalInput").ap()
    pm = nc.dram_tensor("pm", [MIXC // 128, 128, NTOK], F32, kind="ExternalOutput").ap()
    with contextlib.ExitStack() as st:
        env, sb = _common(nc, st)
        gs = sb("gs", [128, KC])
        P = PX(nc)
        P.memset(env["ones"][:], 1.0); P.memset(env["eps"][:], 1e-6)
        P.ld("sync", gs[:], gain)
        nT = env["nT"]
        job = 0
        for half in range(2):
            _rmsnorm(P, nc, env, hT, gs, nT, half)
            for m in range(MIXC // 128):
                j = job % NWB; job += 1
                ps = env["PSA"] if job % 2 == 0 else env["PSB"]
                _wload(P, env["wt"][j], w[:, m * 128:(m + 1) * 128], KC)
                for k in range(KC):
                    for n in range(3):
                        P.mm(ps[:, n, 0:NW], env["wt"][j][:, k, :], nT[:, k, n * NW:(n + 1) * NW], start=(k == 0), stop=(k == KC - 1),
                             r=[env["wt"][j], ("nT", k)])
                P.act(env["ot"][job % 2][:].rearrange("p (n f) -> p n f", n=3), ps[:, :, 0:NW], AF.Copy)
                P.ld("sync", pm[m, :, half * HALF:(half + 1) * HALF], env["ot"][job % 2][:])
        P.emit()
    return nc


def p3_body(P, nc, st, hT, zload, gains, wg, wbo, wout, wfg, wfu, wfd, hmid, hout, hfin):
    last = hfin is not None
    if True:
        env, sb = _common(nc, st)
        gs = sb("gs", [128, 3, KC])
        UN = sb("UN", [128, JC, HALF], BF16)
        gsig = sb("gsig", [128, HALF]); acc = sb("acc", [128, HALF]); prod = sb("prod", [128, HALF])
        P.memset(env["ones"][:], 1.0); P.memset(env["eps"][:], 1e-6)
        P.ld("sync", gs[:], gains)
        nT = env["nT"]; PSA = env["PSA"]; PSB = env["PSB"]; wt = env["wt"]; ot = env["ot"]; hs = env["hs"]
        v3 = lambda t: t[:].rearrange("p (n f) -> p n f", n=3)
        wj = 0
        for half in range(2):
            c0 = half * HALF
            _rmsnorm(P, nc, env, hT, gs[:, 0, :], nT, half)
            for k in range(KC):
                zload(k, half, UN[:, k, :], ("un", k))
            for m in range(KC):
                for b in range(4):
                    j = wj % NWB; wj += 1
                    _wload(P, wt[j], wg[:, b * D + m * 128: b * D + (m + 1) * 128], KC)
                    for k in range(KC):
                        for n in range(3):
                            P.mm(PSA[:, n, 0:NW], wt[j][:, k, :], nT[:, k, n * NW:(n + 1) * NW], start=(k == 0), stop=(k == KC - 1),
                                 r=[wt[j], ("nT", k)])
                    P.act(v3(gsig), PSA[:, :, 0:NW], AF.Sigmoid)
                    j = wj % NWB; wj += 1
                    _wload(P, wt[j], wbo[b, :, m * 128:(m + 1) * 128], 4)
                    for k in range(4):
                        for n in range(3):
                            P.mm(PSB[:, n, 0:NW], wt[j][:, k, :], UN[:, b * 4 + k, n * NW:(n + 1) * NW], start=(k == 0), stop=(k == 3),
                                 r=[wt[j], ("un", b * 4 + k)])
                    if b == 0:
                        P.tt(v3(acc), v3(gsig), PSB[:, :, 0:NW], ALU.mult)
                    else:
                        P.tt(v3(prod), v3(gsig), PSB[:, :, 0:NW], ALU.mult)
                        if b < 3:
                            P.tt(acc[:], acc[:], prod[:], ALU.add)
                        else:
                            P.tt(UN[:, 16 + m, :], acc[:], prod[:], ALU.add, w=[("un", 16 + m)])
            for m in range(KC):
                j = wj % NWB; wj += 1
                _wload(P, wt[j], wout[:, m * 128:(m + 1) * 128], KC)
                for k in range(KC):
                    for n in range(3):
                        P.mm(PSA[:, n, 0:NW], wt[j][:, k, :], UN[:, 16 + k, n * NW:(n + 1) * NW], start=(k == 0), stop=(k == KC - 1),
                             r=[wt[j], ("un", 16 + k)])
                P.ld("sync", hs[m % 2][:], hT[:, m, c0:c0 + HALF])
                P.tt(v3(ot[m % 2]), v3(hs[m % 2]), PSA[:, :, 0:NW], ALU.add)
                P.ld("sync", hmid[:, m, c0:c0 + HALF], ot[m % 2][:])
            _rmsnorm(P, nc, env, hmid, gs[:, 1, :], nT, half)
            for jc in range(JC):
                j = wj % NWB; wj += 1
                _wload(P, wt[j], wfg[:, jc * 128:(jc + 1) * 128], KC)
                for k in range(KC):
                    for n in range(3):
                        P.mm(PSA[:, n, 0:NW], wt[j][:, k, :], nT[:, k, n * NW:(n + 1) * NW], start=(k == 0), stop=(k == KC - 1),
                             r=[wt[j], ("nT", k)])
                P.act(v3(gsig), PSA[:, :, 0:NW], AF.Silu)
                j = wj % NWB; wj += 1
                _wload(P, wt[j], wfu[:, jc * 128:(jc + 1) * 128], KC)
                for k in range(KC):
                    for n in range(3):
                        P.mm(PSB[:, n, 0:NW], wt[j][:, k, :], nT[:, k, n * NW:(n + 1) * NW], start=(k == 0), stop=(k == KC - 1),
                             r=[wt[j], ("nT", k)])
                P.tt(UN[:, jc, :].rearrange("p (n f) -> p n f", n=3), v3(gsig), PSB[:, :, 0:NW], ALU.mult,
                     r=[gsig, PSB] + [("un", 16 + k) for k in range(KC)] + ([("un", jc)] if jc < 32 else []), w=[("un", jc)])
            for m in range(KC):
                bufs = []
                for gq in range(4):
                    j = wj % NWB; wj += 1
                    _wload(P, wt[j], wfd[gq * 11 * 128:(gq + 1) * 11 * 128, m * 128:(m + 1) * 128], 11)
                    bufs.append(j)
                for k in range(JC):
                    jb = bufs[k // 11]
                    for n in range(3):
                        P.mm(PSA[:, n, 0:NW], wt[jb][:, k % 11, :], UN[:, k, n * NW:(n + 1) * NW], start=(k == 0), stop=(k == JC - 1),
                             r=[wt[jb], ("un", k)])
                P.ld("sync", hs[m % 2][:], hmid[:, m, c0:c0 + HALF])
                P.tt(v3(ot[m % 2]), v3(hs[m % 2]), PSA[:, :, 0:NW], ALU.add)
                P.ld("sync", hout[:, m, c0:c0 + HALF], ot[m % 2][:])
            if last:
                _rmsnorm(P, nc, env, hout, gs[:, 2, :], nT, half, fin=hfin)


def build_p3(nc, last):
    hT = nc.dram_tensor("hT", [128, KC, NTOK], F32, kind="ExternalInput").ap()
    zT = nc.dram_tensor("zT", [128, KC, NTOK], F32, kind="ExternalInput").ap()
    gains = nc.dram_tensor("gains", [128, 3, KC], F32, kind="ExternalInput").ap()
    wg = nc.dram_tensor("wg", [D, 4 * D], F32, kind="ExternalInput").ap()
    wbo = nc.dram_tensor("wbo", [4, 512, D], F32, kind="ExternalInput").ap()
    wout = nc.dram_tensor("wout", [D, D], F32, kind="ExternalInput").ap()
    wfg = nc.dram_tensor("wfg", [D, DFF], F32, kind="ExternalInput").ap()
    wfu = nc.dram_tensor("wfu", [D, DFF], F32, kind="ExternalInput").ap()
    wfd = nc.dram_tensor("wfd", [DFF, D], F32, kind="ExternalInput").ap()
    hmid = nc.dram_tensor("hmid", [128, KC, NTOK], F32).ap()
    hout = nc.dram_tensor("hout", [128, KC, NTOK], F32, kind="ExternalOutput").ap()
    hfin = nc.dram_tensor("hfin", [128, KC, NTOK], F32, kind="ExternalOutput").ap() if last else None
    with contextlib.ExitStack() as st:
        P = PX(nc)

        def zload(k, half, dst, key):
            P.ld("pool", dst, zT[:, k, half * HALF:(half + 1) * HALF], w=[key])
        p3_body(P, nc, st, hT, zload, gains, wg, wbo, wout, wfg, wfu, wfd, hmid, hout, hfin)
        P.emit()
    return nc


def build_mixer_prog(layer):
    nc = bass.Bass("TRN2", target_bir_lowering=False)
    pc = nc.dram_tensor("pc", [14, 128, T], F32, kind="ExternalInput").ap()
    pt = nc.dram_tensor("pt", [4, T, 128], F32, kind="ExternalInput").ap()
    shapes = mixer_param_shapes()
    prm = {k: nc.dram_tensor(k, list(v), F32, kind="ExternalInput").ap() for k, v in shapes.items()}
    za = nc.dram_tensor("za", [128, T], F32, kind="ExternalOutput").ap()
    zb = nc.dram_tensor("zb", [128, T], F32, kind="ExternalOutput").ap()
    zc = nc.dram_tensor("zc", [T, 128], F32, kind="ExternalOutput").ap()
    zd = nc.dram_tensor("zd", [T, 128], F32, kind="ExternalOutput").ap()
    with contextlib.ExitStack() as st:
        big = alloc_big(nc, st); psb = alloc_psum(nc, st); sm = alloc_small(nc, st)
        P = PX(nc)
        mixer_A(P, nc, st, big, psb, sm, pc, prm, za)
        P.barrier()
        mixer_C(P, nc, st, big, psb, sm, pc, pt, prm, zc)
        P.barrier()
        mixer_D(P, nc, st, big, psb, sm, pc, pt, prm, zd, layer)
        P.barrier()
        mixer_B(P, nc, st, big, psb, sm, pc, prm, zb)
        P.emit()
    return nc


def mixer_param_shapes():
    return {k: v.shape for k, v in _mixer_params_example().items()}


_EX = {}


def _mixer_params_example():
    if not _EX:
        fake = dict(lru_conv_w=np.zeros((2, 4, 512), np.float32), lru_conv_b=np.zeros((2, 512), np.float32),
                    lru_wa=np.zeros((2, 2, 8, 64, 64), np.float32), lru_wx=np.zeros((2, 2, 8, 64, 64), np.float32),
                    lru_ba=np.zeros((2, 2, 512), np.float32), lru_bx=np.zeros((2, 2, 512), np.float32),
                    lru_lambda=np.zeros((2, 2, 512), np.float32), hgrn_lb_logits=np.zeros((2, 512), np.float32),
                    hy_conv_w=np.zeros((2, 3, 1536), np.float32), hy_conv_b=np.zeros((2, 1536), np.float32),
                    hy_w1=np.zeros((2, 33, 64), np.float32), hy_b1=np.zeros((2, 64), np.float32),
                    hy_w2=np.zeros((2, 64, 64), np.float32), hy_b2=np.zeros((2, 64), np.float32),
                    hy_w3=np.zeros((2, 64, 1024), np.float32), hy_decay=np.zeros((2, 1024), np.float32),
                    hy_bias=np.zeros((2, 512), np.float32))
        _EX.update(mixer_params(fake, 0, 0))
    return _EX


def mixer_params(inp, l, g):
    d = {}
    d.update(prep_A(inp, l, g)); d.update(prep_C(g)); d.update(prep_D(inp, l, g)); d.update(prep_B(inp, l, g))
    return d


def _run(nc, in_maps):
    res = run_bass_kernel_spmd(nc, in_maps, core_ids=list(range(8)))
    return res.results


def _to_fm(a2d):
    nt, nf = a2d.shape
    return np.ascontiguousarray(a2d.T.reshape(nf // 128, 128, nt).transpose(1, 0, 2))


def kernel(**inp):
    inp = {k: np.asarray(v) for k, v in inp.items()}
    x, meta = inp["x"], inp["meta"]
    B = x.shape[0]
    h = np.concatenate([np.broadcast_to(meta[None], (B, 16, D)), x], axis=1)
    hT = [_to_fm(h[c // 4, (c % 4) * NTOK:(c % 4 + 1) * NTOK]) for c in range(8)]
    zref = None
    for l in range(2):
        gfm = lambda v: np.ascontiguousarray(v.reshape(KC, 128).T)
        nc = bass.Bass("TRN2", target_bir_lowering=False)
        build_p1(nc)
        w1 = np.ascontiguousarray(inp["w_in"][l][:, :MIXC])
        g1 = gfm(inp["norm_mix"][l])
        r1 = _run(nc, [dict(hT=hT[c], gain=g1, w=w1) for c in range(8)])
        pmT = [np.concatenate([r1[b * 4 + q]["pm"].reshape(MIXC, NTOK) for q in range(4)], axis=1) for b in range(B)]
        ncm = build_mixer_prog(l)
        ims = []
        for c in range(8):
            b, g = c // 4, c % 4
            pc = np.ascontiguousarray(np.stack([pmT[b][512 * i + 128 * g:512 * i + 128 * (g + 1)] for i in range(14)]))
            pt = np.ascontiguousarray(np.stack([pmT[b][512 * i + 128 * g:512 * i + 128 * (g + 1)].T for i in (7, 8, 12, 13)]))
            d = dict(pc=pc, pt=pt); d.update(mixer_params(inp, l, g))
            ims.append(d)
        rm = _run(ncm, ims)
        zT = []
        for c in range(8):
            b, q = c // 4, c % 4
            ts_ = slice(q * NTOK, (q + 1) * NTOK)
            z = np.empty((128, KC, NTOK), np.float32)
            for g in range(4):
                r = rm[b * 4 + g]
                zb_ = r["zb"]
                if zref and l == 0:
                    zb_ = np.load(zref, mmap_mode="r")[b][:, g * 128:(g + 1) * 128].T
                z[:, 0 + g, :] = r["za"][:, ts_]
                z[:, 4 + g, :] = zb_[:, ts_]
                z[:, 8 + g, :] = r["zc"][ts_].T
                z[:, 12 + g, :] = r["zd"][ts_].T
            zT.append(z)
        last = (l == 1)
        nc3 = bass.Bass("TRN2", target_bir_lowering=False)
        build_p3(nc3, last)
        gains = np.ascontiguousarray(np.stack([gfm(inp["norm_mix"][l]), gfm(inp["norm_ffn"][l]), gfm(inp["norm_final"])], axis=1))
        com = dict(gains=gains, wg=np.ascontiguousarray(inp["w_in"][l][:, MIXC:]), wbo=inp["w_branch_out"][l], wout=inp["w_out"][l],
                   wfg=inp["ffn_w_gate"][l], wfu=inp["ffn_w_up"][l], wfd=inp["ffn_w_down"][l])
        r3 = _run(nc3, [dict(hT=hT[c], zT=zT[c], **com) for c in range(8)])
        hT = [r3[c]["hout"] for c in range(8)]
        if last:
            fin = [r3[c]["hfin"] for c in range(8)]
    out = np.empty((B, 8192, D), np.float32)
    for b in range(B):
        full = np.concatenate([fin[b * 4 + q].transpose(1, 0, 2).reshape(D, NTOK) for q in range(4)], axis=1)
        out[b] = full[:, 16:].T
    return out


PT_IDX = (7, 8, 12, 13)


_PIDC = {}


def _pid_bq(e):
    key = id(e)
    if key not in _PIDC:
        pid = e.partition_id()
        _PIDC[key] = (pid // 4, pid % 4)
    return _PIDC[key]


def phase_norm_gather(P, nc, h_dram, gain_ap, nb, nall):
    with contextlib.ExitStack() as st:
        env, sb = _common(nc, st)
        gs = sb("gs1", [128, KC])
        P.memset(env["ones"][:], 1.0); P.memset(env["eps"][:], 1e-6)
        P.ld("sync", gs[:], gain_ap)
        nbv = nb.rearrange("(k p) c -> p k c", p=128)
        for half in range(2):
            _rmsnorm(P, nc, env, h_dram, gs, env["nT"], half)
            P.ld("sync", nbv[:, :, half * HALF:(half + 1) * HALF], env["nT"][:], r=[("nT", k) for k in range(KC)], w=[nb])
        P.allgather(nall, nb, [list(range(8))])
        P.flush()


def phase_proj(P, nc, w_in_l, nall, nmine, pc, pt):
    with contextlib.ExitStack() as st:
        sb = lambda name, shape, dt=F32: st.enter_context(nc.sbuf_tensor(name + SFX[0], shape, dt))
        wts = sb("p_w", [128, 14, KC, 128], BF16)
        nr = [sb("p_n%d" % i, [128, KC, HALF], BF16) for i in range(2)]
        ot = [sb("p_o%d" % i, [128, HALF]) for i in range(2)]
        ott = [sb("p_ot%d" % i, [128, 128]) for i in range(2)]
        ps = [st.enter_context(nc.psum_tensor("p_ps%d" % i + SFX[0], [128, 3, 512], F32)) for i in range(2)]
        pst = [st.enter_context(nc.psum_tensor("p_pst%d" % i + SFX[0], [128, 128], F32)) for i in range(2)]
        for i in range(14):
            P.ld("pool", wts[:, i, :, :], w_in_l[:, i * 128:(i + 1) * 128].rearrange("(k p) m -> p k m", p=128), w=[("pw", i)])

        def cpn(e):
            b, q = _pid_bq(e)
            return e.dma_start(out=nmine, in_=nall[bass.ds(b * (4 * D), 4 * D), :])
        P.op("sync", cpn, [nall.tensor.name], [nmine.tensor.name], dma=True)
        job = 0; tj = 0
        for r in range(4):
            for half in range(2):
                jn = (r * 2 + half) % 2

                P.ld("sync", nr[jn][:], nmine[r * D:(r + 1) * D, half * HALF:(half + 1) * HALF].rearrange("(k p) c -> p k c", p=128))
                t0 = r * NTOK + half * HALF
                for i in range(14):
                    j = job % 2; job += 1
                    for k in range(KC):
                        for n in range(3):
                            P.mm(ps[j][:, n, 0:NW], wts[:, i, k, :], nr[jn][:, k, n * NW:(n + 1) * NW], start=(k == 0), stop=(k == KC - 1),
                                 r=[("pw", i), nr[jn]])
                    P.act(ot[j][:].rearrange("p (n f) -> p n f", n=3), ps[j][:, :, 0:NW], AF.Copy)
                    P.ld("sync", pc[i, :, t0:t0 + HALF], ot[j][:])
                    if i in PT_IDX:
                        pi = PT_IDX.index(i)
                        for blk in range(9):
                            m = min(128, HALF - blk * 128)
                            jt = tj % 2; tj += 1
                            for k in range(KC):
                                P.mm(pst[jt][0:m, :], nr[jn][:, k, blk * 128:blk * 128 + m], wts[:, i, k, :], start=(k == 0), stop=(k == KC - 1),
                                     r=[("pw", i), nr[jn]])
                            P.cp(ott[jt][0:m, :], pst[jt][0:m, :])
                            P.ld("act", pt[pi, t0 + blk * 128:t0 + blk * 128 + m, :], ott[jt][0:m, :])
        P.flush()


def phase_mixers(P, nc, layer, pc, pt, prm, zd_, zown, zall):
    za_d, zb_d, zc_d, zd_d = zd_
    with contextlib.ExitStack() as st:
        big = alloc_big(nc, st); psb = alloc_psum(nc, st); sm = alloc_small(nc, st)
        mixer_A(P, nc, st, big, psb, sm, pc, prm, za_d)
        P.barrier()
        mixer_C(P, nc, st, big, psb, sm, pc, pt, prm, zc_d)
        P.barrier()
        mixer_D(P, nc, st, big, psb, sm, pc, pt, prm, zd_d, layer)
        P.barrier()
        mixer_B(P, nc, st, big, psb, sm, pc, prm, zb_d)
        P.barrier()
        zbf = sm["rmb"]
        ident = sm["p128"][0]
        P.ld("sync", ident[:], prm["ident"])
        for br, src in ((0, za_d), (1, zb_d)):
            P.ld("pool", zbf[:, 0:T], src)
            P.ld("sync", zown[br * 128:(br + 1) * 128, :], zbf[:, 0:T])
        for br, src in ((2, zc_d), (3, zd_d)):
            X = big[0]
            X3 = X[:, 0:64 * 128].rearrange("p (n d) -> p n d", d=128)
            P.ld("sync", X3, src[0:8192, :].rearrange("(n p) d -> p n d", p=128))
            P.ld("sync", big[1][0:16, 0:128], src[8192:T, :])
            for n in range(65):
                j = n % 2
                if n < 64:
                    P.tr(psb[j][:, 0:128], X3[:, n, :], ident[:])
                    P.cp(zbf[:, n * 128:(n + 1) * 128], psb[j][:, 0:128], eng=("act" if n % 2 else "dve"))
                else:
                    P.tr(psb[j][:, 0:16], big[1][0:16, 0:128], ident[0:16, 0:16])
                    P.cp(zbf[:, 8192:T], psb[j][:, 0:16])
            P.ld("sync", zown[br * 128:(br + 1) * 128, :], zbf[:, 0:T])
        P.allgather(zall[0:8 * 512, :], zown, [list(range(8))])
        P.flush()


def phase_dense(P, nc, h_in, zall, zmine, gains, W, hmid, hout, hfin):
    with contextlib.ExitStack() as st:
        def cpz(e):
            b, q = _pid_bq(e)
            off = b * (2048 * T) + q * NTOK
            src = zall.rearrange("r t -> (r t)")[bass.ds(off, 2048 * T)].rearrange("(r t) -> r t", t=T)[:, 0:NTOK]
            return e.dma_start(out=zmine, in_=src)
        P.op("sync", cpz, [zall.tensor.name], [zmine.tensor.name], dma=True)

        def zload(k, half, dst, key):
            if k % 4 != 0:
                return
            un = dst.tensor
            br = k // 4
            src = zmine.rearrange("(g x) t -> x g t", x=512)[br * 128:(br + 1) * 128, :, half * HALF:(half + 1) * HALF]
            P.ld("sync", un[:, br * 4:(br + 1) * 4, :], src, w=[("un", br * 4 + kk) for kk in range(4)])
        p3_body(P, nc, st, h_in, zload, gains, W["wg"], W["wbo"], W["wout"], W["wfg"], W["wfu"], W["wfd"], hmid, hout, hfin)
        P.flush()


def build_fused(nc):
    ei = lambda name, shape: nc.dram_tensor(name, list(shape), F32, kind="ExternalInput").ap()
    hT0 = ei("hT0", [128, KC, NTOK])
    w_mix = ei("w_mix", [2, D, 14 * 128]); w_gate = ei("w_gate", [2, D, 4 * D]); wbo = ei("w_branch_out", [2, 4, 512, D]); wout = ei("w_out", [2, D, D])
    wfg = ei("ffn_w_gate", [2, D, DFF]); wfu = ei("ffn_w_up", [2, D, DFF]); wfd = ei("ffn_w_down", [2, DFF, D])
    g1 = ei("g_mix", [2, 128, KC]); g3 = ei("g_all", [2, 128, 3, KC])
    shapes = mixer_param_shapes()
    prm = [{k: ei("%s_L%d" % (k, l), v) for k, v in shapes.items()} for l in range(2)]
    hfin = nc.dram_tensor("hfin", [128, KC, NTOK], F32, kind="ExternalOutput").ap()
    it = lambda name, shape, dt=F32: nc.dram_tensor(name, list(shape), dt).ap()
    nb = it("x_nb", [D, NTOK], BF16); nall = it("x_nall", [8 * D, NTOK], BF16)
    pc = it("x_pc", [14, 128, T]); pt = it("x_pt", [4, T, 128])
    zds = (it("x_za", [128, T]), it("x_zb", [128, T]), it("x_zc", [T, 128]), it("x_zd", [T, 128]))
    nmine = it("x_nmine", [4 * D, NTOK], BF16); zmine = it("x_zmine", [2048, NTOK], BF16)
    zown = it("x_zown", [512, T], BF16); zall = it("x_zall", [8 * 512 + 1024, T], BF16)
    hmid = it("x_hmid", [128, KC, NTOK]); h1 = it("x_h1", [128, KC, NTOK]); h2 = it("x_h2", [128, KC, NTOK])
    with contextlib.ExitStack() as gst:
        P = PX(nc)
        P.setup(gst)
        h_in = hT0
        ph = 0
        for l in range(2):
            SFX[0] = "_p%d" % ph; ph += 1
            phase_norm_gather(P, nc, h_in, g1[l], nb, nall)
            SFX[0] = "_p%d" % ph; ph += 1
            phase_proj(P, nc, w_mix[l], nall, nmine, pc, pt)
            SFX[0] = "_p%d" % ph; ph += 1
            phase_mixers(P, nc, l, pc, pt, prm[l], zds, zown, zall)
            SFX[0] = "_p%d" % ph; ph += 1
            W = dict(wg=w_gate[l], wbo=wbo[l], wout=wout[l], wfg=wfg[l], wfu=wfu[l], wfd=wfd[l])
            hout = h1 if l == 0 else h2
            phase_dense(P, nc, h_in, zall, zmine, g3[l], W, hmid, hout, hfin if l == 1 else None)
            h_in = hout
        SFX[0] = ""
    return nc


def kernel(**inp):
    inp = {k: np.asarray(v) for k, v in inp.items()}
    x, meta = inp["x"], inp["meta"]
    B = x.shape[0]
    h = np.concatenate([np.broadcast_to(meta[None], (B, 16, D)), x], axis=1)
    gfm = lambda v: np.ascontiguousarray(v.reshape(KC, 128).T)
    g_mix = np.ascontiguousarray(np.stack([gfm(inp["norm_mix"][l]) for l in range(2)]))
    g_all = np.ascontiguousarray(np.stack([np.stack([gfm(inp["norm_mix"][l]), gfm(inp["norm_ffn"][l]), gfm(inp["norm_final"])], axis=1)
                                           for l in range(2)]))
    w_gate = np.ascontiguousarray(inp["w_in"][:, :, MIXC:])
    common = dict(w_gate=w_gate, w_branch_out=inp["w_branch_out"], w_out=inp["w_out"], ffn_w_gate=inp["ffn_w_gate"],
                  ffn_w_up=inp["ffn_w_up"], ffn_w_down=inp["ffn_w_down"], g_mix=g_mix, g_all=g_all)
    ims = []
    for c in range(8):
        b, q = c // 4, c % 4
        d = dict(common)
        d["hT0"] = _to_fm(h[b, q * NTOK:(q + 1) * NTOK])
        d["w_mix"] = np.ascontiguousarray(np.concatenate([inp["w_in"][:, :, 512 * i + 128 * q:512 * i + 128 * (q + 1)] for i in range(14)], axis=2))
        for l in range(2):
            for k, v in mixer_params(inp, l, q).items():
                d["%s_L%d" % (k, l)] = v
        ims.append(d)
    nc = bass.Bass("TRN2", target_bir_lowering=False)
    build_fused(nc)
    res = run_bass_kernel_spmd(nc, ims, core_ids=list(range(8))).results
    out = np.empty((B, 8192, D), np.float32)
    for b in range(B):
        full = np.concatenate([res[b * 4 + q]["hfin"].transpose(1, 0, 2).reshape(D, NTOK) for q in range(4)], axis=1)
        out[b] = full[:, 16:].T
    return out
```
